# Optimizing a Trainium2 kernel written in Bass

```python
import math
import jax, jax.numpy as jnp
from jax import lax
import numpy as np

D_MODEL = 1024
BATCH = 8
SEQ = 8192
DEPTH = 1

RET_HEADS = 4
RET_DK = 64
RET_DV = 128
RET_QK = RET_HEADS * RET_DK
RET_V = RET_HEADS * RET_DV
CHUNK = 128
FOX_HEADS = 8
FOX_DH = 64
FOX_W = FOX_HEADS * FOX_DH
Q_BLOCK = 128
D_FF = -(-8 * D_MODEL // (3 * 256)) * 256
ROPE_BASE = 10000.0
EPS = 1e-6
IN_SIZES = (RET_QK, RET_QK, RET_V, RET_V, FOX_W, FOX_W, FOX_W, FOX_HEADS, D_MODEL, D_MODEL)
IN_COLS = sum(IN_SIZES)

kernel_name = "hybrid_retention_fox_gated_block"


def rmsnorm(x, g):
    xf = x.astype(jnp.float32)
    y = xf * lax.rsqrt(jnp.mean(xf * xf, axis=-1, keepdims=True) + EPS)
    return (y * g.astype(jnp.float32)).astype(x.dtype)


def rotary(x, pos):
    half = x.shape[-1] // 2
    inv_freq = 1.0 / (ROPE_BASE ** (jnp.arange(half, dtype=jnp.float32) / half))
    ang = pos[:, None] * inv_freq[None, :]
    cos = jnp.cos(ang)[None, :, None, :]
    sin = jnp.sin(ang)[None, :, None, :]
    xf = x.astype(jnp.float32)
    x1, x2 = xf[..., :half], xf[..., half:]
    return jnp.concatenate([x1 * cos - x2 * sin, x1 * sin + x2 * cos], axis=-1)


def retention_chunkwise(q, k, v):
    B, S, H, dk = q.shape
    dv = v.shape[-1]
    n = S // CHUNK
    log_g = jnp.log1p(-(2.0 ** (-5.0 - jnp.arange(H, dtype=jnp.float32))))
    qc = q.astype(jnp.float32).reshape(B, n, CHUNK, H, dk)
    kc = k.astype(jnp.float32).reshape(B, n, CHUNK, H, dk)
    vc = v.astype(jnp.float32).reshape(B, n, CHUNK, H, dv)
    idx = jnp.arange(CHUNK, dtype=jnp.float32)
    diff = idx[:, None] - idx[None, :]
    decay = jnp.where(diff[None] >= 0, jnp.exp(jnp.maximum(diff, 0.0)[None] * log_g[:, None, None]), 0.0)
    scores = jnp.einsum('bnihd,bnjhd->bnhij', qc, kc) * decay[None, None]
    intra = jnp.einsum('bnhij,bnjhe->bnihe', scores, vc)
    zeta = jnp.exp((CHUNK - 1.0 - idx)[None, :] * log_g[:, None])
    chunk_kv = jnp.einsum('bnjhd,hj,bnjhe->bnhde', kc, zeta, vc)
    g_chunk = jnp.exp(CHUNK * log_g)[:, None, None]

    def step(r_prev, kv):
        return g_chunk * r_prev + kv, r_prev

    r0 = jnp.zeros((B, H, dk, dv), jnp.float32)
    _, states = lax.scan(step, r0, jnp.moveaxis(chunk_kv, 1, 0))
    states = jnp.moveaxis(states, 0, 1)
    xi = jnp.exp((idx + 1.0)[None, :] * log_g[:, None]).T
    inter = jnp.einsum('bnihd,bnhde->bnihe', qc, states) * xi[None, None, :, :, None]
    return (intra + inter).reshape(B, S, H, dv)


def forgetting_attention(q, k, v, log_f):
    B, S, H, d = q.shape
    nb = S // Q_BLOCK
    scale = 1.0 / math.sqrt(d)
    c = jnp.cumsum(log_f, axis=1).transpose(0, 2, 1)
    qh = q.transpose(0, 2, 1, 3)
    kh = k.transpose(0, 2, 1, 3)
    vh = v.transpose(0, 2, 1, 3)
    qb = qh.reshape(B, H, nb, Q_BLOCK, d).transpose(2, 0, 1, 3, 4)
    cb = c.reshape(B, H, nb, Q_BLOCK).transpose(2, 0, 1, 3)
    pos_k = jnp.arange(S)

    def block(args):
        i, q_blk, c_blk = args
        s = jnp.einsum('bhqd,bhkd->bhqk', q_blk, kh).astype(jnp.float32) * scale
        s = s + c_blk[..., None] - c[:, :, None, :]
        pos_q = i * Q_BLOCK + jnp.arange(Q_BLOCK)
        mask = pos_k[None, :] <= pos_q[:, None]
        s = jnp.where(mask[None, None], s, -jnp.inf)
        p = jax.nn.softmax(s, axis=-1)
        return jnp.einsum('bhqk,bhkd->bhqd', p.astype(vh.dtype), vh)

    out = lax.map(block, (jnp.arange(nb), qb, cb))
    return out.transpose(1, 2, 0, 3, 4).reshape(B, H, S, d).transpose(0, 2, 1, 3)


def split_cols(z):
    offs = np.cumsum(np.array(IN_SIZES))[:-1].tolist()
    return jnp.split(z, offs, axis=-1)


def setup_inputs(seed: int = 0) -> dict:
    key = jax.random.key(seed)
    ks = jax.random.split(key, 16)
    f32 = jnp.float32
    nrm = lambda k, shp: jax.random.normal(k, shp, f32)
    L = DEPTH
    return {
        "x": nrm(ks[0], (BATCH, SEQ, D_MODEL)),
        "g_mix": 1.0 + 0.02 * nrm(ks[1], (L, D_MODEL)),
        "w_in": nrm(ks[2], (L, D_MODEL, IN_COLS)) * D_MODEL ** -0.5,
        "b_forget": 1.0 + 0.5 * nrm(ks[3], (L, FOX_HEADS)),
        "g_ret_norm": 1.0 + 0.02 * nrm(ks[4], (L, RET_V)),
        "w_ret_o": nrm(ks[5], (L, RET_V, D_MODEL)) * RET_V ** -0.5,
        "g_fox_q": 1.0 + 0.02 * nrm(ks[6], (L, FOX_DH)),
        "g_fox_k": 1.0 + 0.02 * nrm(ks[7], (L, FOX_DH)),
        "w_fox_o": nrm(ks[8], (L, FOX_W, D_MODEL)) * FOX_W ** -0.5,
        "w_out": nrm(ks[9], (L, D_MODEL, D_MODEL)) * D_MODEL ** -0.5,
        "g_ffn": 1.0 + 0.02 * nrm(ks[10], (L, D_MODEL)),
        "w_gate": nrm(ks[11], (L, D_MODEL, D_FF)) * D_MODEL ** -0.5,
        "w_up": nrm(ks[12], (L, D_MODEL, D_FF)) * D_MODEL ** -0.5,
        "w_down": nrm(ks[13], (L, D_FF, D_MODEL)) * D_FF ** -0.5,
    }


def reference(x, g_mix, w_in, b_forget, g_ret_norm, w_ret_o, g_fox_q, g_fox_k, w_fox_o,
              w_out, g_ffn, w_gate, w_up, w_down):
    B, S, _ = x.shape
    pos = jnp.arange(S, dtype=jnp.float32)
    for l in range(DEPTH):
        h = rmsnorm(x, g_mix[l])
        z = h @ w_in[l]
        q_r, k_r, v_r, gt_r, q_f, k_f, v_f, f_f, a_r, a_f = split_cols(z)

        q_r = rotary(q_r.reshape(B, S, RET_HEADS, RET_DK), pos)
        k_r = rotary(k_r.reshape(B, S, RET_HEADS, RET_DK), pos) * (RET_DK ** -0.5)
        v_r = v_r.reshape(B, S, RET_HEADS, RET_DV)
        o_r = retention_chunkwise(q_r, k_r, v_r)
        mu = jnp.mean(o_r, axis=-1, keepdims=True)
        var = jnp.mean(jnp.square(o_r - mu), axis=-1, keepdims=True)
        o_r = ((o_r - mu) * lax.rsqrt(var + EPS)).reshape(B, S, RET_V) * g_ret_norm[l].astype(jnp.float32)
        o_r = (jax.nn.silu(gt_r.astype(jnp.float32)) * o_r).astype(x.dtype)
        y_r = o_r @ w_ret_o[l]

        q_f = rmsnorm(q_f.reshape(B, S, FOX_HEADS, FOX_DH), g_fox_q[l])
        k_f = rmsnorm(k_f.reshape(B, S, FOX_HEADS, FOX_DH), g_fox_k[l])
        v_f = v_f.reshape(B, S, FOX_HEADS, FOX_DH)
        log_f = jax.nn.log_sigmoid(f_f.astype(jnp.float32) + b_forget[l].astype(jnp.float32))
        o_f = forgetting_attention(q_f, k_f, v_f, log_f).reshape(B, S, FOX_W).astype(x.dtype)
        y_f = o_f @ w_fox_o[l]

        merged = (jax.nn.sigmoid(a_r.astype(jnp.float32)) * y_r.astype(jnp.float32)
                  + jax.nn.sigmoid(a_f.astype(jnp.float32)) * y_f.astype(jnp.float32)).astype(x.dtype)
        x = x + merged @ w_out[l]

        h2 = rmsnorm(x, g_ffn[l])
        ff = (jax.nn.silu(h2 @ w_gate[l]) * (h2 @ w_up[l])) @ w_down[l]
        x = x + ff
    return x
```

```python
import contextlib
import os
import sys
import numpy as np
import concourse.bass as bass
import concourse.mybir as mybir
from concourse.bass_utils import run_bass_kernel_spmd

F32 = mybir.dt.float32
BF16 = mybir.dt.bfloat16
AF = mybir.ActivationFunctionType
ALU = mybir.AluOpType
AX = mybir.AxisListType

D = 1024
DFF = 2816
EPS = 1e-6
NWIN = 5640
NCST = 1940


class Tracker:
    ENGS = ("pe", "act", "dve", "pool", "sp")

    def __init__(self):
        self.ops = []
        self.last_writer = {}
        self.readers = {}
        self.last_on_eng = {}
        self.dma_last = {}
        self.dma_count = {}
        self.barrier_deps = {e: set() for e in self.ENGS}
        self.iname = {}

    def add(self, eng, emit, reads=(), writes=(), dma=False):
        idx = len(self.ops)
        deps = set(self.barrier_deps[eng])
        self.barrier_deps[eng] = set()
        for r in reads:
            w = self.last_writer.get(r)
            if w is not None:
                deps.add(w)
        for wr in writes:
            w = self.last_writer.get(wr)
            if w is not None:
                deps.add(w)
            for rd in self.readers.get(wr, ()):
                deps.add(rd)
        key = None
        if dma:
            if len(writes) == 0 or str(writes[0]).startswith("hbm"):
                key = ("st", reads[0])
            else:
                key = ("ld", writes[0])
            self.dma_count[key] = self.dma_count.get(key, 0) + 1
        op = dict(eng=eng, emit=emit, deps=deps, dma=dma, key=key, line=sys._getframe(1).f_lineno,
                  kval=(16 * self.dma_count[key] if dma else None), sig=False, sval=None)
        self.ops.append(op)
        for r in reads:
            lst = self.readers.setdefault(r, [])
            if not dma:
                lst[:] = [i for i in lst if self.ops[i]["dma"] or self.ops[i]["eng"] != eng]
            lst.append(idx)
        for wr in writes:
            self.last_writer[wr] = idx
            self.readers[wr] = []
        self.last_on_eng[eng] = idx
        if dma:
            self.dma_last[key] = idx
        return idx

    def barrier(self):
        s = set(self.last_on_eng.values()) | set(self.dma_last.values())
        for e in self.ENGS:
            self.barrier_deps[e] |= s

    def prepare(self, sem_alloc):
        ops = self.ops
        for op in ops:
            nd = set()
            for d in op["deps"]:
                dop = ops[d]
                if (not dop["dma"]) and (not op["dma"]) and dop["eng"] == op["eng"] == "pe":
                    continue
                nd.add(d)
                if not dop["dma"]:
                    dop["sig"] = True
            op["deps"] = nd
        cnt = {e: 0 for e in self.ENGS}
        for op in ops:
            if op["dma"]:
                continue
            if op["sig"]:
                cnt[op["eng"]] += 1
                op["sval"] = cnt[op["eng"]]
        esem = {e: sem_alloc("sem_" + e) for e in self.ENGS}
        ksem = {}
        for op in ops:
            if op["dma"] and op["key"] not in ksem:
                ksem[op["key"]] = sem_alloc("dk%d" % len(ksem))
        per_eng = {e: [] for e in self.ENGS}
        for i, op in enumerate(ops):
            per_eng[op["eng"]].append(i)

        def run_engine(ename, eng):
            seen = {}
            for i in per_eng[ename]:
                op = ops[i]
                for d in sorted(op["deps"]):
                    dop = ops[d]
                    if dop["dma"]:
                        sem, val, sid = ksem[dop["key"]], dop["kval"], ("k", dop["key"])
                    else:
                        sem, val, sid = esem[dop["eng"]], dop["sval"], ("e", dop["eng"])
                    if seen.get(sid, 0) >= val:
                        continue
                    eng.wait_ge(sem, val)
                    seen[sid] = val
                ins = op["emit"](eng)
                try:
                    self.iname[ins.ins.name] = op["line"]
                except Exception:
                    pass
                if op["dma"]:
                    ins.then_inc(ksem[op["key"]], 16)
                elif op["sig"]:
                    ins.then_inc(esem[ename], 1)
            if ename == "sp":
                for key, n in self.dma_count.items():
                    eng.wait_ge(ksem[key], 16 * n)

        return run_engine


class Arena:
    def __init__(self, big, nbytes):
        self.big = big
        self.n = nbytes
        self.off = 0

    def alloc(self, shape, dtype):
        esz = 4 if dtype == F32 else 2
        nel = int(np.prod(shape))
        nb = (nel * esz + 63) // 64 * 64
        assert self.off + nb <= self.n, ("arena overflow", self.off, nb, self.n)
        a = self.off // 2
        b = a + nel * esz // 2
        self.off += nb
        v = self.big[:, a:b]
        if dtype == F32:
            v = v.bitcast(F32)
        if len(shape) == 2:
            v = v.rearrange("p (a b) -> p a b", a=shape[0])
        elif len(shape) == 3:
            v = v.rearrange("p (a b c) -> p a b c", a=shape[0], b=shape[1])
        return v


def mm(out, lhsT, rhs, start, stop):
    return lambda e: e.matmul(out, lhsT=lhsT, rhs=rhs, start=start, stop=stop)


def tr(out, in_, ident):
    return lambda e: e.transpose(out=out, in_=in_, identity=ident)


def act(out, in_, func, bias=None, scale=None, accum=None):
    kw = {}
    if bias is not None:
        kw["bias"] = bias
    if scale is not None:
        kw["scale"] = scale
    if accum is not None:
        kw["accum_out"] = accum
    return lambda e: e.activation(out=out, in_=in_, func=func, **kw)


def tt(out, a, b, op):
    return lambda e: e.tensor_tensor(out=out, in0=a, in1=b, op=op)


def ts(out, a, s1, op0, s2=None, op1=None):
    if op1 is None:
        return lambda e: e.tensor_scalar(out=out, in0=a, scalar1=s1, scalar2=None, op0=op0)
    return lambda e: e.tensor_scalar(out=out, in0=a, scalar1=s1, scalar2=s2, op0=op0, op1=op1)


def stt(out, a, s, b, op0, op1):
    return lambda e: e.scalar_tensor_tensor(out=out, in0=a, scalar=s, in1=b, op0=op0, op1=op1)


def cp(out, in_):
    return lambda e: e.tensor_copy(out=out, in_=in_)


def acp(out, in_):
    return lambda e: e.copy(out=out, in_=in_)


def rcp(out, in_):
    return lambda e: e.reciprocal(out=out, in_=in_)


def dma(out, in_):
    return lambda e: e.dma_start(out=out, in_=in_)


def mset(ap, v):
    return lambda e: e.memset(ap, v)


def build_nc(S):
    NT = S // 512
    NJ = S // 128
    nc = bass.Bass("TRN2", target_bir_lowering=False)

    def din(name, shape, dt=F32):
        return nc.dram_tensor(name, shape, dt, kind="ExternalInput").ap()

    x_d = din("x", [S, D])
    win_d = din("win", [D, NWIN])
    wro_d = din("wro", [512, D])
    wfo_d = din("wfo", [512, D])
    wout_d = din("wout", [D, D])
    wg_d = din("wg", [D, DFF])
    wu_d = din("wu", [D, DFF])
    wd_d = din("wd", [DFF, D])
    gmt_d = din("gmt", [128, D])
    gft_d = din("gft", [128, D])
    cst_d = din("cst", [128, NCST])
    esel_d = din("esel", [24, 1024])
    rot_d = din("rot", [4, 128, S])
    out_d = nc.dram_tensor("out", [S, D], F32, kind="ExternalOutput").ap()
    qf_s = nc.dram_tensor("qf_s", [512, S], BF16, kind="Internal").ap()
    kf_s = nc.dram_tensor("kf_s", [512, S], BF16, kind="Internal").ap()
    vf_s = nc.dram_tensor("vf_s", [S, 520], BF16, kind="Internal").ap()
    gaf_s = nc.dram_tensor("gaf_s", [D, S], BF16, kind="Internal").ap()
    qr_s = nc.dram_tensor("qr_s", [256, S], BF16, kind="Internal").ap()
    kr_s = nc.dram_tensor("kr_s", [256, S], BF16, kind="Internal").ap()
    qx_s = nc.dram_tensor("qx_s", [256, S], BF16, kind="Internal").ap()
    gt_s = nc.dram_tensor("gt_s", [512, S], BF16, kind="Internal").ap()
    vr_s = nc.dram_tensor("vr_s", [S, 512], BF16, kind="Internal").ap()
    sar_s = nc.dram_tensor("sar_s", [D, S], BF16, kind="Internal").ap()
    ofr_s = nc.dram_tensor("ofr_s", [512, S], BF16, kind="Internal").ap()
    cq_s = nc.dram_tensor("cq_s", [24, S], BF16, kind="Internal").ap()
    of_s = nc.dram_tensor("of_s", [512, S], BF16, kind="Internal").ap()

    T = Tracker()
    ARENA_BYTES = 207 * 1024
    with contextlib.ExitStack() as st:
        big = st.enter_context(nc.sbuf_tensor("big", [128, ARENA_BYTES // 2], BF16))
        PB = [st.enter_context(nc.psum_tensor("pb%d" % i, [128, 512], F32)) for i in range(8)]
        PBb = [p[:, :].bitcast(BF16) for p in PB]
        A = Arena(big, ARENA_BYTES)

        identb = A.alloc([128], BF16)
        bonesb = A.alloc([128], BF16)
        phaseC_start = A.off
        cst = A.alloc([NCST], F32)
        identf = cst[:, 0:128]
        negtri = cst[:, 128:256]
        bonesf = cst[:, 256:384]
        decay = cst[:, 384:896]
        xiT = cst[:, 896:1920].rearrange("p (c n) -> p c n", c=2)
        gret = cst[:, 1920:1924]
        gq = cst[:, 1924:1925]
        gk = cst[:, 1925:1926]
        gch = cst[:, 1926:1928]
        zeta = cst[:, 1928:1932]
        bfor = cst[:, 1932:1940]
        eselb = A.alloc([8, 128], BF16)
        negmask = A.alloc([128], BF16)
        negones = A.alloc([128], F32)
        onesf = A.alloc([128], F32)
        cK = A.alloc([NJ, 8], F32)
        mref = A.alloc([NT, 8], F32)
        carry = [A.alloc([8], F32) for _ in range(2)]
        persist_end = A.off
        eself = A.alloc([1024], F32)

        T.add("sp", dma(cst, cst_d), writes=["cst"], dma=True)
        T.add("pool", mset(eself, 0.0), writes=["eself"])
        T.add("sp", dma(eself[0:24, :], esel_d), writes=["eself"], dma=True)
        T.add("dve", cp(identb, identf), reads=["cst"], writes=["identb"])
        T.add("dve", cp(bonesb, bonesf), reads=["cst"], writes=["bonesb"])
        T.add("dve", cp(eselb.rearrange("p a b -> p (a b)"), eself), reads=["eself"], writes=["eselb"])
        T.add("dve", ts(negmask, negtri, 1.0, ALU.add, -30000.0, ALU.mult), reads=["cst"], writes=["negmask"])
        T.add("pool", mset(negones, -1.0), writes=["negones"])
        T.add("pool", mset(onesf, 1.0), writes=["onesf"])
        T.add("pool", mset(carry[0], 0.0), writes=["carry0"])

        pbc = [0]

        def nb(n=8, base=0):
            b = base + pbc[0] % n
            pbc[0] += 1
            return b

        win = A.alloc([8, NWIN], BF16)
        gmt = A.alloc([D], F32)
        xb = [A.alloc([D], F32) for _ in range(2)]
        junk = A.alloc([D], BF16)
        ss = A.alloc([4], F32)
        rs = A.alloc([4], F32)
        xn = A.alloc([4, D], BF16)
        hTb = [A.alloc([8, 512], BF16) for _ in range(2)]
        HTR = [["hT0", "hT1", "hT2", "hT3"], ["hU0", "hU1", "hU2", "hU3"]]
        hT = hTb[0]
        rot = A.alloc([4, 512], F32)
        t1 = [A.alloc([512], F32) for _ in range(2)]
        t2 = [A.alloc([512], F32) for _ in range(2)]
        qrT = A.alloc([2, 512], BF16)
        krT = A.alloc([2, 512], BF16)
        qxT = A.alloc([2, 512], BF16)
        qrz = A.alloc([4, 512], BF16)
        gtT = A.alloc([4, 512], BF16)
        sqf = [A.alloc([512], BF16) for _ in range(2)]
        rsf = [A.alloc([512], F32) for _ in range(2)]
        stg = [A.alloc([512], BF16) for _ in range(4)]
        stgv = [A.alloc([8, 65], BF16) for _ in range(2)]
        vr = A.alloc([4, 512], BF16)
        sgt = [A.alloc([512], BF16) for _ in range(2)]
        fb = A.alloc([8], F32)
        fe = A.alloc([8], F32)
        fsp = A.alloc([8], F32)
        d8 = A.alloc([8], F32)
        r1 = A.alloc([8], F32)
        r2 = A.alloc([8], F32)
        hml = A.alloc([24], BF16)
        cqst = A.alloc([512], BF16)
        kz = A.alloc([256], BF16)
        sT = A.alloc([512], BF16)
        stf = A.alloc([2, 128], F32)
        stb = A.alloc([4, 128], BF16)
        s1 = A.alloc([4], F32)
        s2 = A.alloc([4], F32)
        mean = A.alloc([4], F32)
        msq = A.alloc([4], F32)
        var = A.alloc([4], F32)
        rstd = A.alloc([4], F32)
        nbias = A.alloc([4], F32)
        on = A.alloc([512], BF16)
        ofT = A.alloc([4, 512], BF16)
        mrs = [A.alloc([512], F32) for _ in range(2)]

        win_v = win_d.rearrange("(c p) n -> p c n", p=128)
        for n0 in range(0, NWIN, 1880):
            T.add("pool", dma(win[:, :, n0:n0 + 1880], win_v[:, :, n0:n0 + 1880]), writes=["win%d" % n0], dma=True)
        WIN_RES = ["win%d" % n0 for n0 in range(0, NWIN, 1880)]
        T.add("sp", dma(gmt, gmt_d), writes=["gmt"], dma=True)
        for i in range(2):
            T.add("pool", mset(stgv[i], 1.0), writes=["stgv%d" % i])

        C_QR, C_QS, C_KR, C_KS, C_GT, C_QF, C_KF, C_AR, C_AF, C_VR, C_VF, C_FF = (
            0, 256, 512, 768, 1024, 1536, 2048, 2560, 3584, 4608, 5120, 5632)
        HT_RES = ["hT0", "hT1", "hT2", "hT3"]

        def norm_pre(xbuf, xres, s, gtab, gres, ring):
            k = s % ring
            T.add("act", act(junk, xbuf, AF.Square, accum=ss[:, s:s + 1]), reads=[xres], writes=["ss%d" % s])
            T.add("act", act(rs[:, s:s + 1], ss[:, s:s + 1], AF.Sqrt, bias=EPS, scale=1.0 / D),
                  reads=["ss%d" % s], writes=["rs%d" % s])
            T.add("dve", rcp(rs[:, s:s + 1], rs[:, s:s + 1]), reads=["rs%d" % s], writes=["rs%d" % s])
            T.add("dve", stt(xn[:, k, :], xbuf, rs[:, s:s + 1], gtab, ALU.mult, ALU.mult),
                  reads=[xres, "rs%d" % s, gres], writes=["xn%d" % k])

        def norm_tr(s, hbuf, hres, ring):
            k = s % ring
            b = nb()
            for c in range(8):
                T.add("pe", tr(PBb[b][:, c * 128:(c + 1) * 128], xn[:, k, c * 128:(c + 1) * 128], identb),
                      reads=["xn%d" % k, "identb"], writes=["pb%d" % b])
            T.add("act", acp(hbuf[:, :, s * 128:(s + 1) * 128], PBb[b].rearrange("p (c n) -> p c n", c=8)),
                  reads=["pb%d" % b], writes=[hres[s]])

        def norm_transpose(xbuf, xres, s, gtab, gres, hbuf, hres):
            norm_pre(xbuf, xres, s, gtab, gres, 2)
            norm_tr(s, hbuf, hres, 2)

        def proj_fm(col0, b, wt, wres, width=128):
            for c in range(8):
                T.add("pe", mm(PB[b][:, :], wt[:, c, col0:col0 + width], hT[:, c, :], c == 0, c == 7),
                      reads=wres + HT_RES, writes=["pb%d" % b])

        def load_xsub(buf, res, src, j):
            if j < NJ:
                T.add("sp", dma(buf, src[j * 128:(j + 1) * 128, :]), writes=[res], dma=True)

        stgc = [0]

        def next_stg():
            i = stgc[0] % 4
            stgc[0] += 1
            return i

        load_xsub(xb[0], "xb0", x_d, 0)
        load_xsub(xb[1], "xb1", x_d, 1)
        for t in range(NT):
            tc0 = t * 512
            hT = hTb[t % 2]
            HT_RES = HTR[t % 2]
            T.add("sp", dma(rot, rot_d[:, :, tc0:tc0 + 512].rearrange("a p n -> p a n")), writes=["rot"], dma=True)
            if t == 0:
                for s in range(4):
                    norm_pre(xb[s % 2], "xb%d" % (s % 2), s, gmt, "gmt", 4)
                    load_xsub(xb[s % 2], "xb%d" % (s % 2), x_d, s + 2)
            for s in range(4):
                norm_tr(s, hT, HT_RES, 4)
            for (cm, cs, ci, dst, dres) in ((C_QR, C_QS, 0, qrT, "qrT"), (C_KR, C_KS, 2, krT, "krT")):
                for c2 in range(2):
                    ba = nb()
                    proj_fm(cm + c2 * 128, ba, win, WIN_RES)
                    bb = nb()
                    proj_fm(cs + c2 * 128, bb, win, WIN_RES)
                    T.add("dve", tt(t1[c2], PB[ba][:, :], rot[:, ci, :], ALU.mult), reads=["pb%d" % ba, "rot"], writes=["t1%d" % c2])
                    T.add("dve", tt(t2[c2], PB[bb][:, :], rot[:, ci + 1, :], ALU.mult), reads=["pb%d" % bb, "rot"], writes=["t2%d" % c2])
                    T.add("pool", tt(dst[:, c2, :], t1[c2], t2[c2], ALU.add), reads=["t1%d" % c2, "t2%d" % c2], writes=["%s%d" % (dres, c2)])
                    if ci == 0:
                        T.add("pool", tt(qxT[:, c2, :], qrT[:, c2, :], xiT[:, c2, :], ALU.mult),
                              reads=["qrT%d" % c2, "cst"], writes=["qxT%d" % c2])
            v2 = lambda d_: d_.rearrange("(c p) s -> p c s", p=128)[:, :, tc0:tc0 + 512]
            T.add("sp", dma(v2(qr_s), qrT), reads=["qrT0", "qrT1"], dma=True)
            T.add("sp", dma(v2(kr_s), krT), reads=["krT0", "krT1"], dma=True)
            T.add("sp", dma(v2(qx_s), qxT), reads=["qxT0", "qxT1"], dma=True)
            for g in range(4):
                b = nb()
                proj_fm(C_GT + g * 128, b, win, WIN_RES)
                T.add("act", act(gtT[:, g, :], PB[b][:, :], AF.Silu), reads=["pb%d" % b], writes=["gtT"])
            T.add("sp", dma(v2(gt_s), gtT), reads=["gtT"], dma=True)
            for s in range(4):
                b = nb()
                for c in range(8):
                    T.add("pe", mm(PB[b][:, :], hT[:, c, s * 128:(s + 1) * 128], win[:, c, C_VR:C_VR + 512], c == 0, c == 7),
                          reads=WIN_RES + [HT_RES[s]], writes=["pb%d" % b])
                T.add("dve", cp(vr[:, s, :], PB[b][:, :]), reads=["pb%d" % b], writes=["vr%d" % s])
                T.add("sp", dma(vr_s[(t * 4 + s) * 128:(t * 4 + s + 1) * 128, :], vr[:, s, :]), reads=["vr%d" % s], dma=True)

            def gen_qf():
                pend = None

                def finish_qk(p):
                    (b, i, g, gcol, dstd) = p
                    b2 = nb()
                    T.add("pe", mm(PB[b2][:, :], bonesb, sqf[i], True, True), reads=["bonesb", "sqf%d" % i], writes=["pb%d" % b2])
                    T.add("act", act(rsf[i], PB[b2][:, :], AF.Sqrt, bias=EPS, scale=1.0 / 64), reads=["pb%d" % b2], writes=["rsf%d" % i])
                    T.add("dve", rcp(rsf[i], rsf[i]), reads=["rsf%d" % i], writes=["rsf%d" % i])
                    k = next_stg()
                    T.add("dve", stt(stg[k], PB[b][:, :], gcol, rsf[i], ALU.mult, ALU.mult),
                          reads=["pb%d" % b, "rsf%d" % i, "cst"], writes=["stg%d" % k])
                    T.add("sp", dma(dstd[g * 128:(g + 1) * 128, tc0:tc0 + 512], stg[k]), reads=["stg%d" % k], dma=True)

                for idx in range(8):
                    g = idx % 4
                    col = (C_QF if idx < 4 else C_KF) + g * 128
                    b = nb()
                    proj_fm(col, b, win, WIN_RES)
                    i = idx % 2
                    T.add("act", act(sqf[i], PB[b][:, :], AF.Square), reads=["pb%d" % b], writes=["sqf%d" % i])
                    if pend is not None:
                        finish_qk(pend)
                    pend = (b, i, g, gq if idx < 4 else gk, qf_s if idx < 4 else kf_s)
                    yield
                finish_qk(pend)
                yield
            def gen_af():
                for g in range(8):
                    b = nb()
                    proj_fm(C_AF + g * 128, b, win, WIN_RES)
                    k = next_stg()
                    T.add("act", act(stg[k], PB[b][:, :], AF.Sigmoid), reads=["pb%d" % b], writes=["stg%d" % k])
                    T.add("sp", dma(gaf_s[g * 128:(g + 1) * 128, tc0:tc0 + 512], stg[k]), reads=["stg%d" % k], dma=True)
                    yield
            def gen_tm():
                for s in range(4):
                    J = t * 4 + s
                    b = nb()
                    for c in range(8):
                        T.add("pe", mm(PB[b][:, :], hT[:, c, s * 128:(s + 1) * 128], win[:, c, C_VF:C_VF + 512], c == 0, c == 7),
                              reads=WIN_RES + [HT_RES[s]], writes=["pb%d" % b])
                    kv = J % 2
                    T.add("act", acp(stgv[kv][:, :, 0:64], PB[b][:, :].rearrange("p (h d) -> p h d", h=8)),
                          reads=["pb%d" % b], writes=["stgv%d" % kv])
                    T.add("sp", dma(vf_s[J * 128:(J + 1) * 128, :], stgv[kv].rearrange("p h d -> p (h d)")),
                          reads=["stgv%d" % kv], dma=True)
                    yield
                    b = nb()
                    for c in range(8):
                        T.add("pe", mm(PB[b][:, 0:8], hT[:, c, s * 128:(s + 1) * 128], win[:, c, C_FF:C_FF + 8], c == 0, c == 7),
                              reads=WIN_RES + [HT_RES[s]], writes=["pb%d" % b])
                    T.add("dve", tt(fb, PB[b][:, 0:8], bfor, ALU.add), reads=["pb%d" % b, "cst"], writes=["fb"])
                    T.add("act", act(fe, fb, AF.Exp, scale=-1.0), reads=["fb"], writes=["fe"])
                    T.add("act", act(fsp, fe, AF.Ln, bias=1.0), reads=["fe"], writes=["fsp"])
                    yield
                    b = nb()
                    T.add("pe", mm(PB[b][:, 0:8], negtri, fsp, True, True), reads=["cst", "fsp"], writes=["pb%d" % b])
                    T.add("pe", mm(PB[b][:, 8:16], negones, fsp, True, True), reads=["negones", "fsp"], writes=["pb%d" % b])
                    ca, cb = carry[J % 2], carry[(J + 1) % 2]
                    car, cbr = "carry%d" % (J % 2), "carry%d" % ((J + 1) % 2)
                    if s == 0:
                        T.add("dve", cp(mref[:, t, :], ca), reads=[car], writes=["mref"])
                    T.add("dve", tt(cK[:, J, :], PB[b][:, 0:8], ca, ALU.add), reads=["pb%d" % b, car], writes=["cK"])
                    T.add("dve", tt(cb, PB[b][:, 8:16], ca, ALU.add), reads=["pb%d" % b, car], writes=[cbr])
                    T.add("dve", tt(d8, cK[:, J, :], mref[:, t, :], ALU.subtract), reads=["cK", "mref"], writes=["d8"])
                    T.add("dve", ts(d8, d8, 8.0, ALU.mult), reads=["d8"], writes=["d8"])
                    T.add("dve", cp(hml[:, 0:8], d8), reads=["d8"], writes=["hml"])
                    T.add("dve", tt(r1, d8, hml[:, 0:8], ALU.subtract), reads=["d8", "hml"], writes=["r1"])
                    T.add("dve", cp(hml[:, 8:16], r1), reads=["r1"], writes=["hml"])
                    T.add("dve", tt(r2, r1, hml[:, 8:16], ALU.subtract), reads=["r1", "hml"], writes=["r2"])
                    T.add("dve", cp(hml[:, 16:24], r2), reads=["r2"], writes=["hml"])
                    yield
                    b = nb()
                    T.add("pe", tr(PBb[b][0:24, 0:128], hml, identb), reads=["hml", "identb"], writes=["pb%d" % b])
                    T.add("dve", cp(cqst[0:24, s * 128:(s + 1) * 128], PBb[b][0:24, 0:128]), reads=["pb%d" % b], writes=["cqst"])
                    yield
                T.add("sp", dma(cq_s[:, tc0:tc0 + 512], cqst[0:24, :]), reads=["cqst"], dma=True)
            def gen_ar():
                for g in range(8):
                    b = nb()
                    proj_fm(C_AR + g * 128, b, win, WIN_RES)
                    k = next_stg()
                    T.add("act", act(stg[k], PB[b][:, :], AF.Sigmoid), reads=["pb%d" % b], writes=["stg%d" % k])
                    T.add("sp", dma(sar_s[g * 128:(g + 1) * 128, tc0:tc0 + 512], stg[k]), reads=["stg%d" % k], dma=True)
                    yield

            def chain(*gs):
                for g_ in gs:
                    yield from g_

            def gen_norm():
                if t + 1 < NT:
                    for s in range(4):
                        norm_transpose(xb[s % 2], "xb%d" % (s % 2), s, gmt, "gmt", hTb[(t + 1) % 2], HTR[(t + 1) % 2])
                        load_xsub(xb[s % 2], "xb%d" % (s % 2), x_d, (t + 1) * 4 + s + 2)
                        yield

            gens = [chain(gen_qf(), gen_af(), gen_ar()), gen_tm()]
            alive = [True, True]
            while any(alive):
                for gi in (0, 0, 1):
                    if alive[gi]:
                        alive[gi] = next(gens[gi], 'done') != 'done'
            if t + 1 < NT:
                for s in range(4):
                    norm_pre(xb[s % 2], "xb%d" % (s % 2), s, gmt, "gmt", 4)
                    load_xsub(xb[s % 2], "xb%d" % (s % 2), x_d, (t + 1) * 4 + s + 2)

        T.barrier()
        n_ops_A = len(T.ops)
        A.off = persist_end
        KT = A.alloc([4, S], BF16)
        V = A.alloc([NJ, 520], BF16)
        qTb = [A.alloc([8, 512], BF16) for _ in range(2)]
        cqb = [A.alloc([512], BF16) for _ in range(2)]
        bias = A.alloc([NJ, 8], F32)
        pbuf = [A.alloc([512], BF16) for _ in range(6)]
        den = A.alloc([512], F32)
        bcs = A.alloc([512], F32)
        ostg = [A.alloc([512], BF16) for _ in range(2)]
        qrz = A.alloc([4, 512], BF16)
        krT = A.alloc([2, 512], BF16)
        qxT = A.alloc([2, 512], BF16)
        gtT = A.alloc([4, 512], BF16)
        vr = A.alloc([4, 512], BF16)
        kz = A.alloc([256], BF16)
        sT = A.alloc([512], BF16)
        stf = A.alloc([2, 128], F32)
        stb = A.alloc([4, 128], BF16)
        osb = A.alloc([512], F32)
        sqo = A.alloc([512], F32)
        s1 = A.alloc([4], F32)
        s2 = A.alloc([4], F32)
        mean = A.alloc([4], F32)
        msq = A.alloc([4], F32)
        var = A.alloc([4], F32)
        rstd = A.alloc([4], F32)
        mhalf = A.alloc([4], F32)
        on = A.alloc([512], BF16)
        ofT = A.alloc([4, 512], BF16)
        T.add("pool", mset(stf, 0.0), writes=["stf"])
        T.add("pool", mset(stb, 0.0), writes=["stb"])
        T.add("pool", mset(qrz, 0.0), writes=["qrze", "qrzo"])
        T.add("pool", mset(mhalf, -0.5), writes=["mhalf"])
        qrz4 = qrz.rearrange("p (c two) n -> p c two n", two=2)
        qr4 = qr_s.rearrange("(c two d) s -> d c two s", two=2, d=64)

        for i_ in range(2):
            T.add("pool", mset(qTb[i_], 0.0), writes=["qTe%d" % i_, "qTo%d" % i_])
            T.add("pool", mset(cqb[i_], 0.0), writes=["cq%d" % i_])
        qT4b = [q_.rearrange("p (c two) n -> p c two n", two=2) for q_ in qTb]

        def load_q(G_):
            if G_ < NT:
                i_ = G_ % 2
                c0_ = G_ * 512
                T.add("sp", dma(qT4b[i_][0:64, :, 0, :], qf4[:, :, 0, c0_:c0_ + 512]), writes=["qTe%d" % i_], dma=True)
                T.add("sp", dma(qT4b[i_][64:128, :, 1, :], qf4[:, :, 1, c0_:c0_ + 512]), writes=["qTo%d" % i_], dma=True)
                T.add("sp", dma(cqb[i_][0:24, :], cq_s[:, c0_:c0_ + 512]), writes=["cq%d" % i_], dma=True)

        qf4 = qf_s.rearrange("(c two d) s -> d c two s", two=2, d=64)
        KT_RES = ["KT%d" % hc for hc in range(4)]
        for hc in range(4):
            T.add("sp", dma(KT[:, hc, :], kf_s[hc * 128:(hc + 1) * 128, :]), writes=[KT_RES[hc]], dma=True)
        vf_v = vf_s.rearrange("(j p) f -> p j f", p=128)
        V_RES = []
        for j0 in range(0, NJ, 8):
            j1 = min(NJ, j0 + 8)
            T.add("sp", dma(V[:, j0:j1, :], vf_v[:, j0:j1, :]), writes=["V%d" % j0], dma=True)
            V_RES.append("V%d" % j0)

        LA = 2
        SB = [0, 1, 2]
        def gen_ret_all():
            for RG in range(NT):
                rt0 = RG * 512
                v2 = lambda d_: d_.rearrange("(c p) s -> p c s", p=128)[:, :, rt0:rt0 + 512]
                T.add("sp", dma(qrz4[0:64, :, 0, :], qr4[:, :, 0, rt0:rt0 + 512]), writes=["qrze"], dma=True)
                T.add("sp", dma(qrz4[64:128, :, 1, :], qr4[:, :, 1, rt0:rt0 + 512]), writes=["qrzo"], dma=True)
                T.add("sp", dma(krT, v2(kr_s)), writes=["krT"], dma=True)
                T.add("sp", dma(qxT, v2(qx_s)), writes=["qxT"], dma=True)
                T.add("sp", dma(gtT, v2(gt_s)), writes=["gtT"], dma=True)
                T.add("sp", dma(vr, vr_s.rearrange("(j p) f -> p j f", p=128)[:, RG * 4:RG * 4 + 4, :]), writes=["vr"], dma=True)
                yield
                for s in range(4):
                    cs_ = slice(s * 128, (s + 1) * 128)
                    for c2 in range(2):
                        T.add("pe", tr(PBb[7][:, c2 * 128:(c2 + 1) * 128], krT[:, c2, cs_], identb),
                              reads=["krT", "identb"], writes=["pb7"])
                    T.add("dve", tt(kz.rearrange("p (h d) -> p h d", h=4), PBb[7][:, 0:256].rearrange("p (h d) -> p h d", h=4),
                                    zeta.unsqueeze(2).broadcast_to([128, 4, 64]), ALU.mult),
                          reads=["pb7", "cst"], writes=["kz"])
                    for h in range(4):
                        c2 = h // 2
                        T.add("pe", mm(PB[6][:, h * 128:(h + 1) * 128], krT[:, c2, cs_], qrz[:, h, cs_], True, True),
                              reads=["krT", "qrze" if h % 2 == 0 else "qrzo"], writes=["pb6"])
                    yield
                    T.add("dve", tt(sT, PB[6][:, :], decay, ALU.mult), reads=["pb6", "cst"], writes=["sT"])
                    yield
                    for h in range(4):
                        c2 = h // 2
                        T.add("pe", mm(PB[6][:, h * 128:(h + 1) * 128], sT[:, h * 128:(h + 1) * 128], vr[:, s, h * 128:(h + 1) * 128], True, False),
                              reads=["sT", "vr"], writes=["pb6"])
                        T.add("pe", mm(PB[6][:, h * 128:(h + 1) * 128], qxT[:, c2, cs_], stb[:, h, :], False, True),
                              reads=["qxT", "stb"], writes=["pb6"])
                    for h in range(4):
                        c2, po = h // 2, (h % 2) * 64
                        T.add("pe", mm(PB[7][po:po + 64, c2 * 128:(c2 + 1) * 128], kz[:, h * 64:(h + 1) * 64], vr[:, s, h * 128:(h + 1) * 128], True, True),
                              reads=["kz", "vr"], writes=["pb7"])
                    yield
                    T.add("dve", cp(osb, PB[6][:, :]), reads=["pb6"], writes=["osb"])
                    for c2 in range(2):
                        T.add("dve", stt(stf[:, c2, :], stf[:, c2, :], gch[:, c2:c2 + 1], PB[7][:, c2 * 128:(c2 + 1) * 128], ALU.mult, ALU.add),
                              reads=["stf", "pb7", "cst"], writes=["stf"])
                    for c2 in range(2):
                        T.add("pool", cp(stb[0:64, 2 * c2, :], stf[0:64, c2, :]), reads=["stf"], writes=["stb"])
                        T.add("pool", cp(stb[64:128, 2 * c2 + 1, :], stf[64:128, c2, :]), reads=["stf"], writes=["stb"])
                    T.add("pool", tt(sqo, osb, osb, ALU.mult), reads=["osb"], writes=["sqo"])
                    T.add("dve", (lambda o, i_: (lambda e: e.tensor_reduce(out=o, in_=i_, axis=AX.X, op=ALU.add)))(
                        s1, osb.rearrange("p (h e) -> p h e", h=4)), reads=["osb"], writes=["s1"])
                    T.add("dve", ts(mean, s1, 1.0 / 128, ALU.mult), reads=["s1"], writes=["mean"])
                    T.add("dve", tt(msq, mean, mean, ALU.mult), reads=["mean"], writes=["msq"])
                    yield
                    T.add("dve", (lambda o, i_: (lambda e: e.tensor_reduce(out=o, in_=i_, axis=AX.X, op=ALU.add)))(
                        s2, sqo.rearrange("p (h e) -> p h e", h=4)), reads=["sqo"], writes=["s2"])
                    T.add("dve", stt(var, s2, 1.0 / 128, msq, ALU.mult, ALU.subtract), reads=["s2", "msq"], writes=["var"])
                    T.add("dve", ts(var, var, EPS, ALU.add), reads=["var"], writes=["var"])
                    T.add("pool", tt(rstd, var, mhalf, ALU.pow), reads=["var", "mhalf"], writes=["rstd"])
                    yield
                    for h in range(4):
                        T.add("dve", ts(on[:, h * 128:(h + 1) * 128], osb[:, h * 128:(h + 1) * 128], mean[:, h:h + 1], ALU.subtract,
                                        rstd[:, h:h + 1], ALU.mult),
                              reads=["osb", "mean", "rstd"], writes=["on"])
                    yield
                    for h in range(4):
                        T.add("pe", tr(PBb[7][:, 512 + h * 128:512 + (h + 1) * 128], on[:, h * 128:(h + 1) * 128], identb),
                              reads=["on", "identb"], writes=["pb7"])
                    for h in range(4):
                        T.add("dve", stt(ofT[:, h, cs_], PBb[7][:, 512 + h * 128:512 + (h + 1) * 128], gret[:, h:h + 1], gtT[:, h, cs_], ALU.mult, ALU.mult),
                              reads=["pb7", "gtT", "cst"], writes=["ofT"])
                    yield
                T.add("sp", dma(v2(ofr_s), ofT), reads=["ofT"], dma=True)

        g_ret = gen_ret_all()
        ret_state = {'alive': True, 'cnt': 0}
        tot_blocks = sum(8 * (4 * g_ + 4) for g_ in range(NT))
        ret_every = max(1, tot_blocks // (NT * 33 + 8))
        for G in range(NT):
            tc0 = G * 512
            nJ = 4 * G + 4
            if G == 0:
                load_q(0)
            load_q(G + 1)
            qT = qTb[G % 2]
            cq = cqb[G % 2]
            gq_ = G % 2
            T.add("dve", tt(bias[:, 0:nJ, :], mref[:, G, :].unsqueeze(1).broadcast_to([128, nJ, 8]), cK[:, 0:nJ, :], ALU.subtract),
                  reads=["mref", "cK"], writes=["bias"])
            seq = [(h, J) for h in range(8) for J in range(nJ)]
            deferred = []

            def emit_qk(i, h, J):
                hc = h // 2
                q0 = max(0, J - 4 * G) * 128
                b = SB[i % 3]
                pk = i % 6
                diag = J >= 4 * G
                T.add("pe", mm(PB[b][:, q0:512], KT[:, hc, J * 128:(J + 1) * 128], qT[:, h, q0:512], True, False),
                      reads=[KT_RES[hc], ("qTe%d" if h % 2 == 0 else "qTo%d") % gq_], writes=["pb%d" % b])
                T.add("pe", mm(PB[b][:, q0:512], eselb[:, h, :], cq[:, q0:512], False, not diag),
                      reads=["eselb", "cq%d" % gq_], writes=["pb%d" % b])
                if diag:
                    T.add("pe", mm(PB[b][:, q0:q0 + 128], identb, negmask, False, True),
                          reads=["identb", "negmask"], writes=["pb%d" % b])
                T.add("act", act(pbuf[pk][:, q0:512], PB[b][:, q0:512], AF.Exp, bias=bias[:, J, h:h + 1], scale=0.125),
                      reads=["pb%d" % b, "bias"], writes=["pbuf%d" % pk])

            def emit_pv(i, h, J):
                q0 = max(0, J - 4 * G) * 128
                pk = i % 6
                ob = 3 + h % 2
                T.add("pe", mm(PB[ob][0:65, q0:512], V[:, J, h * 65:(h + 1) * 65], pbuf[pk][:, q0:512], J == 0, J == nJ - 1),
                      reads=["V%d" % (J // 8 * 8), "pbuf%d" % pk], writes=["pb%d" % ob])

            def epi1(h):
                ob = 3 + h % 2
                T.add("dve", cp(den[64:65, :], PB[ob][64:65, :]), reads=["pb%d" % ob], writes=["den"])
                T.add("dve", rcp(den[64:65, :], den[64:65, :]), reads=["den"], writes=["den"])

            def epi2(h):
                ob = 3 + h % 2
                k = h % 2
                T.add("pe", mm(PB[5][0:64, :], onesf[64:65, 0:64], den[64:65, :], True, True), reads=["onesf", "den"], writes=["pb5"])
                T.add("dve", cp(bcs[0:64, :], PB[5][0:64, :]), reads=["pb5"], writes=["bcs"])
                T.add("dve", tt(ostg[k][0:64, :], PB[ob][0:64, :], bcs[0:64, :], ALU.mult), reads=["pb%d" % ob, "bcs"], writes=["ostg%d" % k])
                T.add("sp", dma(of_s[h * 64:(h + 1) * 64, tc0:tc0 + 512], ostg[k][0:64, :]), reads=["ostg%d" % k], dma=True)

            n = len(seq)
            for i in range(n + LA + 3):
                if i < n:
                    ret_state['cnt'] += 1
                    if ret_state['alive'] and ret_state['cnt'] % ret_every == 0:
                        ret_state['alive'] = next(g_ret, 'done') != 'done'
                    emit_qk(i, *seq[i])
                j = i - LA
                if 0 <= j < n:
                    h, J = seq[j]
                    emit_pv(j, h, J)
                    if J == nJ - 1:
                        epi1(h)
                        deferred.append((i + 2, h))
                while deferred and deferred[0][0] <= i:
                    epi2(deferred.pop(0)[1])
            assert not deferred

        while ret_state['alive']:
            ret_state['alive'] = next(g_ret, 'done') != 'done'

        T.barrier()
        n_ops_B = len(T.ops)
        A.off = phaseC_start
        wg = A.alloc([8, DFF], BF16)
        wu = A.alloc([8, DFF], BF16)
        b2_start = A.off
        wfo = A.alloc([4, D], BF16)
        wro = A.alloc([4, D], BF16)
        wout = A.alloc([8, D], BF16)
        on8 = [A.alloc([4, 512], BF16) for _ in range(2)]
        ofr = [A.alloc([4, 512], BF16) for _ in range(2)]
        gaf = [A.alloc([8, 512], BF16) for _ in range(2)]
        sar = [A.alloc([8, 512], BF16) for _ in range(2)]
        tmpf = [A.alloc([512], F32) for _ in range(2)]
        tmpr = [A.alloc([512], F32) for _ in range(2)]
        mgT = A.alloc([8, 512], BF16)
        xb2 = [A.alloc([D], F32) for _ in range(2)]
        b2_end = A.off
        T.add("pool", dma(wfo, wfo_d.rearrange("(c p) n -> p c n", p=128)), writes=["wfo"], dma=True)
        T.add("pool", dma(wro, wro_d.rearrange("(c p) n -> p c n", p=128)), writes=["wro"], dma=True)
        T.add("pool", dma(wout, wout_d.rearrange("(c p) n -> p c n", p=128)), writes=["wout"], dma=True)
        wg_v = wg_d.rearrange("(c p) n -> p c n", p=128)
        wu_v = wu_d.rearrange("(c p) n -> p c n", p=128)
        WG_RES, WU_RES = [], []
        for n0 in range(0, DFF, 1408):
            T.add("pool", dma(wg[:, :, n0:n0 + 1408], wg_v[:, :, n0:n0 + 1408]), writes=["wg%d" % n0], dma=True)
            T.add("pool", dma(wu[:, :, n0:n0 + 1408], wu_v[:, :, n0:n0 + 1408]), writes=["wu%d" % n0], dma=True)
            WG_RES.append("wg%d" % n0)
            WU_RES.append("wu%d" % n0)
        def load_b2(G):
            if G < NT:
                c0_ = G * 512
                T.add("sp", dma(on8[G % 2], of_s.rearrange("(c p) s -> p c s", p=128)[:, :, c0_:c0_ + 512]), writes=["on8%d" % (G % 2)], dma=True)
                T.add("sp", dma(gaf[G % 2], gaf_s.rearrange("(c p) s -> p c s", p=128)[:, :, c0_:c0_ + 512]), writes=["gaf%d" % (G % 2)], dma=True)
                T.add("sp", dma(ofr[G % 2], ofr_s.rearrange("(c p) s -> p c s", p=128)[:, :, c0_:c0_ + 512]), writes=["ofr%d" % (G % 2)], dma=True)
                T.add("sp", dma(sar[G % 2], sar_s.rearrange("(c p) s -> p c s", p=128)[:, :, c0_:c0_ + 512]), writes=["sar%d" % (G % 2)], dma=True)

        load_b2(0)
        for G in range(NT):
            tc0 = G * 512
            load_b2(G + 1)
            for dg in range(8):
                b = (2 * dg) % 8
                b2_ = (2 * dg + 1) % 8
                k = dg % 2
                for c in range(4):
                    T.add("pe", mm(PB[b][:, :], wfo[:, c, dg * 128:(dg + 1) * 128], on8[G % 2][:, c, :], c == 0, c == 3),
                          reads=["wfo", "on8%d" % (G % 2)], writes=["pb%d" % b])
                for c in range(4):
                    T.add("pe", mm(PB[b2_][:, :], wro[:, c, dg * 128:(dg + 1) * 128], ofr[G % 2][:, c, :], c == 0, c == 3),
                          reads=["wro", "ofr%d" % (G % 2)], writes=["pb%d" % b2_])
                T.add("dve", tt(tmpf[k], PB[b][:, :], gaf[G % 2][:, dg, :], ALU.mult), reads=["pb%d" % b, "gaf%d" % (G % 2)], writes=["tmpf%d" % k])
                T.add("act", acp(tmpr[k], PB[b2_][:, :]), reads=["pb%d" % b2_], writes=["tmpr%d" % k])
                T.add("pool", tt(tmpr[k], tmpr[k], sar[G % 2][:, dg, :], ALU.mult), reads=["tmpr%d" % k, "sar%d" % (G % 2)], writes=["tmpr%d" % k])
                T.add("pool", tt(mgT[:, dg, :], tmpf[k], tmpr[k], ALU.add), reads=["tmpf%d" % k, "tmpr%d" % k], writes=["mgT"])
            for s in range(4):
                load_xsub(xb2[s % 2], "xb2%d" % (s % 2), x_d, G * 4 + s)
                for half in range(2):
                    b = (2 * s + half) % 8
                    for c in range(8):
                        T.add("pe", mm(PB[b][:, :], mgT[:, c, s * 128:(s + 1) * 128], wout[:, c, half * 512:(half + 1) * 512], c == 0, c == 7),
                              reads=["mgT", "wout"], writes=["pb%d" % b])
                    T.add("dve", tt(xb2[s % 2][:, half * 512:(half + 1) * 512], PB[b][:, :], xb2[s % 2][:, half * 512:(half + 1) * 512], ALU.add),
                          reads=["pb%d" % b, "xb2%d" % (s % 2)], writes=["xb2%d" % (s % 2)])
                r0 = (G * 4 + s) * 128
                T.add("sp", dma(out_d[r0:r0 + 128, :], xb2[s % 2]), reads=["xb2%d" % (s % 2)], dma=True)

        T.barrier()
        n_ops_B2 = len(T.ops)
        A.off = b2_start
        wd = A.alloc([22, D], BF16)
        wd_v = wd_d.rearrange("(c p) n -> p c n", p=128)
        WD_RES = []
        for c0 in range(0, 22, 11):
            T.add("pool", dma(wd[:, c0:c0 + 11, :], wd_v[:, c0:c0 + 11, :]), writes=["wd%d" % c0], dma=True)
            WD_RES.append("wd%d" % c0)
        gft = A.alloc([D], F32)
        xc = [A.alloc([D], F32) for _ in range(2)]
        xr = [A.alloc([D], F32) for _ in range(2)]
        junk = A.alloc([D], BF16)
        ss = A.alloc([4], F32)
        rs = A.alloc([4], F32)
        xn = A.alloc([2, D], BF16)
        hTb = [A.alloc([8, 512], BF16) for _ in range(2)]
        sg = [A.alloc([512], F32) for _ in range(2)]
        hid = A.alloc([22, 512], BF16)

        pass
        T.add("sp", dma(gft, gft_d), writes=["gft"], dma=True)

        pbc[0] = 0
        load_xsub(xc[0], "xc0", out_d, 0)
        load_xsub(xc[1], "xc1", out_d, 1)
        for s in range(4):
            norm_transpose(xc[s % 2], "xc%d" % (s % 2), s, gft, "gft", hTb[0], HTR[0])
            load_xsub(xc[s % 2], "xc%d" % (s % 2), out_d, s + 2)
        for t in range(NT):
            hT = hTb[t % 2]
            HT_RES = HTR[t % 2]
            for fc in range(22):
                if t + 1 < NT and fc in (0, 4, 8, 12):
                    s = (0, 4, 8, 12).index(fc)
                    norm_pre(xc[s % 2], "xc%d" % (s % 2), s, gft, "gft", 2)
                    load_xsub(xc[s % 2], "xc%d" % (s % 2), out_d, (t + 1) * 4 + s + 2)
                if t + 1 < NT and fc in (3, 7, 11, 15):
                    s = (3, 7, 11, 15).index(fc)
                    norm_tr(s, hTb[(t + 1) % 2], HTR[(t + 1) % 2], 2)
                ba = nb(6)
                proj_fm(fc * 128, ba, wg, WG_RES)
                bb = nb(6)
                proj_fm(fc * 128, bb, wu, WU_RES)
                k = fc % 2
                T.add("act", act(sg[k], PB[ba][:, :], AF.Silu), reads=["pb%d" % ba], writes=["sg%d" % k])
                T.add("dve", tt(hid[:, fc, :], PB[bb][:, :], sg[k], ALU.mult), reads=["pb%d" % bb, "sg%d" % k], writes=["hid"])
            for s in range(4):
                k = s % 2
                load_xsub(xr[k], "xr%d" % k, out_d, t * 4 + s)
                for half in range(2):
                    b = 6 + half
                    for fc in range(22):
                        T.add("pe", mm(PB[b][:, :], hid[:, fc, s * 128:(s + 1) * 128], wd[:, fc, half * 512:(half + 1) * 512], fc == 0, fc == 21),
                              reads=["hid"] + WD_RES, writes=["pb%d" % b])
                    T.add("dve", tt(xr[k][:, half * 512:(half + 1) * 512], PB[b][:, :], xr[k][:, half * 512:(half + 1) * 512], ALU.add),
                          reads=["pb%d" % b, "xr%d" % k], writes=["xr%d" % k])
                r0 = (t * 4 + s) * 128
                T.add("sp", dma(out_d[r0:r0 + 128, :], xr[k]), reads=["xr%d" % k], dma=True)

        def sem_alloc(name):
            return st.enter_context(nc.semaphore(name))

        _stop = os.environ.get("K_STOP", "")
        if _stop:
            ncut = {"A": n_ops_A, "B": n_ops_B, "B2": n_ops_B2}.get(_stop, int(_stop) if _stop.isdigit() else len(T.ops))
            del T.ops[ncut:]
            T.dma_count = {}
            for op in T.ops:
                if op["dma"]:
                    T.dma_count[op["key"]] = T.dma_count.get(op["key"], 0) + 1
        run = T.prepare(sem_alloc)
        with nc.Block() as block:
            block.sync(lambda e: run("sp", e))
            block.tensor(lambda e: run("pe", e))
            block.scalar(lambda e: run("act", e))
            block.vector(lambda e: run("dve", e))
            block.gpsimd(lambda e: run("pool", e))
    nc._dbg_iname = T.iname
    return nc


def host_constants(S):
    f64 = np.float64
    pos = np.arange(S, dtype=np.float32)
    half = 32
    inv_freq = (1.0 / (np.float32(10000.0) ** (np.arange(half, dtype=np.float32) / np.float32(half)))).astype(np.float32)
    p = np.arange(128)
    d = p % 64
    ang = (pos[None, :] * inv_freq[d % 32][:, None]).astype(np.float32).astype(f64)
    cos = np.cos(ang)
    sin = np.sin(ang) * np.where(d < 32, -1.0, 1.0)[:, None]
    rot = np.stack([cos, sin, 0.125 * cos, 0.125 * sin]).astype(np.float32)
    log_g = np.log1p(-(2.0 ** (-5.0 - np.arange(4, dtype=f64))))
    i = np.arange(128)
    diff = i[None, :] - i[:, None]
    decay = np.zeros((128, 4, 128), f64)
    for h in range(4):
        decay[:, h, :] = np.where(diff >= 0, np.exp(np.maximum(diff, 0) * log_g[h]), 0.0)
    xi = np.zeros((128, 2, 512), f64)
    for c2 in range(2):
        hd = 2 * c2 + p // 64
        xi[:, c2, :] = np.exp((np.arange(512) % 128 + 1.0)[None, :] * log_g[hd][:, None])
    zeta = np.exp((127.0 - i)[:, None] * log_g[None, :])
    gch = np.stack([np.exp(128.0 * log_g[2 * c2 + p // 64]) for c2 in range(2)], axis=1)
    ident = np.eye(128)
    negtri = np.where(i[:, None] <= i[None, :], -1.0, 0.0)
    bones = np.kron(np.eye(2), np.ones((64, 64)))
    esel = np.zeros((24, 8, 128), np.float32)
    for h in range(8):
        for k in range(3):
            esel[8 * k + h, h, :] = 1.0
    return rot, ident, negtri, bones, decay.reshape(128, 512), xi.reshape(128, 1024), zeta, gch, esel.reshape(24, 1024)


def win_perm():
    def swap(base):
        idx = []
        for h in range(4):
            idx += list(range(base + h * 64 + 32, base + h * 64 + 64)) + list(range(base + h * 64, base + h * 64 + 32))
        return idx
    r = lambda a, n: list(range(a, a + n))
    perm = (r(0, 256) + swap(0) + r(256, 256) + swap(256) + r(1024, 512) + r(1536, 512) + r(2048, 512)
            + r(3080, 1024) + r(4104, 1024) + r(512, 512) + r(2560, 512) + r(3072, 8))
    assert len(perm) == NWIN
    return np.array(perm)


def make_in_maps(S, x, g_mix, w_in, b_forget, g_ret_norm, w_ret_o, g_fox_q, g_fox_k, w_fox_o,
                 w_out, g_ffn, w_gate, w_up, w_down):
    f = lambda a: np.ascontiguousarray(np.asarray(a, dtype=np.float32))
    rot, ident, negtri, bones, decay, xi, zeta, gch, esel = host_constants(S)
    cst = np.zeros((128, NCST), np.float32)
    cst[:, 0:128] = ident
    cst[:, 128:256] = negtri
    cst[:, 256:384] = bones
    cst[:, 384:896] = decay
    cst[:, 896:1920] = xi
    cst[:, 1920:1924] = f(g_ret_norm)[0].reshape(4, 128).T
    cst[:, 1924] = np.tile(f(g_fox_q)[0], 2)
    cst[:, 1925] = np.tile(f(g_fox_k)[0], 2)
    cst[:, 1926:1928] = gch
    cst[:, 1928:1932] = zeta
    cst[:, 1932:1940] = np.broadcast_to(f(b_forget)[0][None, :], (128, 8))
    shared = {
        "win": f(f(w_in)[0][:, win_perm()]),
        "wro": f(w_ret_o)[0], "wfo": f(w_fox_o)[0], "wout": f(w_out)[0],
        "wg": f(w_gate)[0], "wu": f(w_up)[0], "wd": f(w_down)[0],
        "gmt": f(np.broadcast_to(f(g_mix)[0][None, :], (128, D))),
        "gft": f(np.broadcast_to(f(g_ffn)[0][None, :], (128, D))),
        "cst": cst, "esel": f(esel), "rot": f(rot),
    }
    xs = f(x)
    return [dict(shared, x=np.ascontiguousarray(xs[b])) for b in range(xs.shape[0])]


_NC_CACHE = {}


def kernel(**inputs):
    x = np.asarray(inputs["x"])
    B, S, _ = x.shape
    if S not in _NC_CACHE:
        _NC_CACHE[S] = build_nc(S)
    nc = _NC_CACHE[S]
    in_maps = make_in_maps(S, **inputs)
    res = run_bass_kernel_spmd(nc, in_maps, core_ids=list(range(B)))
    return np.stack([np.asarray(r["out"]) for r in res.results], axis=0).astype(np.float32)
```

```python
import contextlib
import os
import sys
import numpy as np
import concourse.bass as bass
import concourse.mybir as mybir
from concourse.bass_utils import run_bass_kernel_spmd

F32 = mybir.dt.float32
BF16 = mybir.dt.bfloat16
AF = mybir.ActivationFunctionType
ALU = mybir.AluOpType
AX = mybir.AxisListType

D = 1024
DFF = 2816
EPS = 1e-6
NWIN = 5640
NCST = 1940


class Tracker:
    ENGS = ("pe", "act", "dve", "pool", "sp")

    def __init__(self):
        self.ops = []
        self.last_writer = {}
        self.readers = {}
        self.last_on_eng = {}
        self.dma_last = {}
        self.dma_count = {}
        self.barrier_deps = {e: set() for e in self.ENGS}
        self.iname = {}

    def add(self, eng, emit, reads=(), writes=(), dma=False):
        idx = len(self.ops)
        deps = set(self.barrier_deps[eng])
        self.barrier_deps[eng] = set()
        for r in reads:
            w = self.last_writer.get(r)
            if w is not None:
                deps.add(w)
        for wr in writes:
            w = self.last_writer.get(wr)
            if w is not None:
                deps.add(w)
            for rd in self.readers.get(wr, ()):
                deps.add(rd)
        key = None
        if dma:
            if len(writes) == 0 or str(writes[0]).startswith("hbm"):
                key = ("st", reads[0])
            else:
                key = ("ld", writes[0])
            self.dma_count[key] = self.dma_count.get(key, 0) + 1
        op = dict(eng=eng, emit=emit, deps=deps, dma=dma, key=key, line=sys._getframe(1).f_lineno,
                  kval=(16 * self.dma_count[key] if dma else None), sig=False, sval=None)
        self.ops.append(op)
        for r in reads:
            lst = self.readers.setdefault(r, [])
            if not dma:
                lst[:] = [i for i in lst if self.ops[i]["dma"] or self.ops[i]["eng"] != eng]
            lst.append(idx)
        for wr in writes:
            self.last_writer[wr] = idx
            self.readers[wr] = []
        self.last_on_eng[eng] = idx
        if dma:
            self.dma_last[key] = idx
        return idx

    def barrier(self):
        s = set(self.last_on_eng.values()) | set(self.dma_last.values())
        for e in self.ENGS:
            self.barrier_deps[e] |= s

    def prepare(self, sem_alloc):
        ops = self.ops
        for op in ops:
            nd = set()
            for d in op["deps"]:
                dop = ops[d]
                if (not dop["dma"]) and (not op["dma"]) and dop["eng"] == op["eng"] == "pe":
                    continue
                nd.add(d)
                if not dop["dma"]:
                    dop["sig"] = True
            op["deps"] = nd
        cnt = {e: 0 for e in self.ENGS}
        for op in ops:
            if op["dma"]:
                continue
            if op["sig"]:
                cnt[op["eng"]] += 1
                op["sval"] = cnt[op["eng"]]
        esem = {e: sem_alloc("sem_" + e) for e in self.ENGS}
        ksem = {}
        for op in ops:
            if op["dma"] and op["key"] not in ksem:
                ksem[op["key"]] = sem_alloc("dk%d" % len(ksem))
        per_eng = {e: [] for e in self.ENGS}
        for i, op in enumerate(ops):
            per_eng[op["eng"]].append(i)

        def run_engine(ename, eng):
            seen = {}
            for i in per_eng[ename]:
                op = ops[i]
                for d in sorted(op["deps"]):
                    dop = ops[d]
                    if dop["dma"]:
                        sem, val, sid = ksem[dop["key"]], dop["kval"], ("k", dop["key"])
                    else:
                        sem, val, sid = esem[dop["eng"]], dop["sval"], ("e", dop["eng"])
                    if seen.get(sid, 0) >= val:
                        continue
                    eng.wait_ge(sem, val)
                    seen[sid] = val
                ins = op["emit"](eng)
                try:
                    self.iname[ins.ins.name] = op["line"]
                except Exception:
                    pass
                if op["dma"]:
                    ins.then_inc(ksem[op["key"]], 16)
                elif op["sig"]:
                    ins.then_inc(esem[ename], 1)
            if ename == "sp":
                for key, n in self.dma_count.items():
                    eng.wait_ge(ksem[key], 16 * n)

        return run_engine


class Arena:
    def __init__(self, big, nbytes):
        self.big = big
        self.n = nbytes
        self.off = 0

    def alloc(self, shape, dtype):
        esz = 4 if dtype == F32 else 2
        nel = int(np.prod(shape))
        nb = (nel * esz + 63) // 64 * 64
        assert self.off + nb <= self.n, ("arena overflow", self.off, nb, self.n)
        a = self.off // 2
        b = a + nel * esz // 2
        self.off += nb
        v = self.big[:, a:b]
        if dtype == F32:
            v = v.bitcast(F32)
        if len(shape) == 2:
            v = v.rearrange("p (a b) -> p a b", a=shape[0])
        elif len(shape) == 3:
            v = v.rearrange("p (a b c) -> p a b c", a=shape[0], b=shape[1])
        return v


def mm(out, lhsT, rhs, start, stop):
    return lambda e: e.matmul(out, lhsT=lhsT, rhs=rhs, start=start, stop=stop)


def tr(out, in_, ident):
    return lambda e: e.transpose(out=out, in_=in_, identity=ident)


def act(out, in_, func, bias=None, scale=None, accum=None):
    kw = {}
    if bias is not None:
        kw["bias"] = bias
    if scale is not None:
        kw["scale"] = scale
    if accum is not None:
        kw["accum_out"] = accum
    return lambda e: e.activation(out=out, in_=in_, func=func, **kw)


def tt(out, a, b, op):
    return lambda e: e.tensor_tensor(out=out, in0=a, in1=b, op=op)


def ts(out, a, s1, op0, s2=None, op1=None):
    if op1 is None:
        return lambda e: e.tensor_scalar(out=out, in0=a, scalar1=s1, scalar2=None, op0=op0)
    return lambda e: e.tensor_scalar(out=out, in0=a, scalar1=s1, scalar2=s2, op0=op0, op1=op1)


def stt(out, a, s, b, op0, op1):
    return lambda e: e.scalar_tensor_tensor(out=out, in0=a, scalar=s, in1=b, op0=op0, op1=op1)


def cp(out, in_):
    return lambda e: e.tensor_copy(out=out, in_=in_)


def acp(out, in_):
    return lambda e: e.copy(out=out, in_=in_)


def rcp(out, in_):
    return lambda e: e.reciprocal(out=out, in_=in_)


def dma(out, in_):
    return lambda e: e.dma_start(out=out, in_=in_)


def mset(ap, v):
    return lambda e: e.memset(ap, v)


def build_nc(S):
    NT = S // 512
    NJ = S // 128
    nc = bass.Bass("TRN2", target_bir_lowering=False)

    def din(name, shape, dt=F32):
        return nc.dram_tensor(name, shape, dt, kind="ExternalInput").ap()

    x_d = din("x", [S, D])
    win_d = din("win", [D, NWIN])
    wro_d = din("wro", [512, D])
    wfo_d = din("wfo", [512, D])
    wout_d = din("wout", [D, D])
    wg_d = din("wg", [D, DFF])
    wu_d = din("wu", [D, DFF])
    wd_d = din("wd", [DFF, D])
    gmt_d = din("gmt", [128, D])
    gft_d = din("gft", [128, D])
    cst_d = din("cst", [128, NCST])
    esel_d = din("esel", [24, 1024])
    rot_d = din("rot", [4, 128, S])
    out_d = nc.dram_tensor("out", [S, D], F32, kind="ExternalOutput").ap()
    qf_s = nc.dram_tensor("qf_s", [512, S], BF16, kind="Internal").ap()
    kf_s = nc.dram_tensor("kf_s", [512, S], BF16, kind="Internal").ap()
    vf_s = nc.dram_tensor("vf_s", [S, 520], BF16, kind="Internal").ap()
    gaf_s = nc.dram_tensor("gaf_s", [D, S], BF16, kind="Internal").ap()
    qr_s = nc.dram_tensor("qr_s", [256, S], BF16, kind="Internal").ap()
    kr_s = nc.dram_tensor("kr_s", [256, S], BF16, kind="Internal").ap()
    qx_s = nc.dram_tensor("qx_s", [256, S], BF16, kind="Internal").ap()
    gt_s = nc.dram_tensor("gt_s", [512, S], BF16, kind="Internal").ap()
    vr_s = nc.dram_tensor("vr_s", [S, 512], BF16, kind="Internal").ap()
    sar_s = nc.dram_tensor("sar_s", [D, S], BF16, kind="Internal").ap()
    ofr_s = nc.dram_tensor("ofr_s", [512, S], BF16, kind="Internal").ap()
    cq_s = nc.dram_tensor("cq_s", [24, S], BF16, kind="Internal").ap()
    of_s = nc.dram_tensor("of_s", [512, S], BF16, kind="Internal").ap()

    T = Tracker()
    ARENA_BYTES = 207 * 1024
    with contextlib.ExitStack() as st:
        big = st.enter_context(nc.sbuf_tensor("big", [128, ARENA_BYTES // 2], BF16))
        PB = [st.enter_context(nc.psum_tensor("pb%d" % i, [128, 512], F32)) for i in range(8)]
        PBb = [p[:, :].bitcast(BF16) for p in PB]
        A = Arena(big, ARENA_BYTES)

        identb = A.alloc([128], BF16)
        bonesb = A.alloc([128], BF16)
        phaseC_start = A.off
        cst = A.alloc([NCST], F32)
        identf = cst[:, 0:128]
        negtri = cst[:, 128:256]
        bonesf = cst[:, 256:384]
        decay = cst[:, 384:896]
        xiT = cst[:, 896:1920].rearrange("p (c n) -> p c n", c=2)
        gret = cst[:, 1920:1924]
        gq = cst[:, 1924:1925]
        gk = cst[:, 1925:1926]
        gch = cst[:, 1926:1928]
        zeta = cst[:, 1928:1932]
        bfor = cst[:, 1932:1940]
        eselb = A.alloc([8, 128], BF16)
        negmask = A.alloc([128], BF16)
        negones = A.alloc([128], F32)
        onesf = A.alloc([128], F32)
        cK = A.alloc([NJ, 8], F32)
        mref = A.alloc([NT, 8], F32)
        carry = [A.alloc([8], F32) for _ in range(2)]
        persist_end = A.off
        eself = A.alloc([1024], F32)

        T.add("sp", dma(cst, cst_d), writes=["cst"], dma=True)
        T.add("pool", mset(eself, 0.0), writes=["eself"])
        T.add("sp", dma(eself[0:24, :], esel_d), writes=["eself"], dma=True)
        T.add("dve", cp(identb, identf), reads=["cst"], writes=["identb"])
        T.add("dve", cp(bonesb, bonesf), reads=["cst"], writes=["bonesb"])
        T.add("dve", cp(eselb.rearrange("p a b -> p (a b)"), eself), reads=["eself"], writes=["eselb"])
        T.add("dve", ts(negmask, negtri, 1.0, ALU.add, -30000.0, ALU.mult), reads=["cst"], writes=["negmask"])
        T.add("pool", mset(negones, -1.0), writes=["negones"])
        T.add("pool", mset(onesf, 1.0), writes=["onesf"])
        T.add("pool", mset(carry[0], 0.0), writes=["carry0"])

        pbc = [0]

        def nb(n=8, base=0):
            b = base + pbc[0] % n
            pbc[0] += 1
            return b

        win = A.alloc([8, NWIN], BF16)
        gmt = A.alloc([D], F32)
        xb = [A.alloc([D], F32) for _ in range(2)]
        junk = A.alloc([D], BF16)
        ss = A.alloc([4], F32)
        rs = A.alloc([4], F32)
        xn = A.alloc([4, D], BF16)
        hTb = [A.alloc([8, 512], BF16) for _ in range(2)]
        HTR = [["hT0", "hT1", "hT2", "hT3"], ["hU0", "hU1", "hU2", "hU3"]]
        hT = hTb[0]
        rot = A.alloc([4, 512], F32)
        t1 = [A.alloc([512], F32) for _ in range(2)]
        t2 = [A.alloc([512], F32) for _ in range(2)]
        qrT = A.alloc([2, 512], BF16)
        krT = A.alloc([2, 512], BF16)
        qxT = A.alloc([2, 512], BF16)
        qrz = A.alloc([4, 512], BF16)
        gtT = A.alloc([4, 512], BF16)
        sqf = [A.alloc([512], BF16) for _ in range(2)]
        rsf = [A.alloc([512], F32) for _ in range(2)]
        stg = [A.alloc([512], BF16) for _ in range(4)]
        stgv = [A.alloc([8, 65], BF16) for _ in range(2)]
        vr = A.alloc([4, 512], BF16)
        sgt = [A.alloc([512], BF16) for _ in range(2)]
        fb = A.alloc([8], F32)
        fe = A.alloc([8], F32)
        fsp = A.alloc([8], F32)
        d8 = A.alloc([8], F32)
        r1 = A.alloc([8], F32)
        r2 = A.alloc([8], F32)
        hml = A.alloc([24], BF16)
        cqst = A.alloc([512], BF16)
        kz = A.alloc([256], BF16)
        sT = A.alloc([512], BF16)
        stf = A.alloc([2, 128], F32)
        stb = A.alloc([4, 128], BF16)
        s1 = A.alloc([4], F32)
        s2 = A.alloc([4], F32)
        mean = A.alloc([4], F32)
        msq = A.alloc([4], F32)
        var = A.alloc([4], F32)
        rstd = A.alloc([4], F32)
        nbias = A.alloc([4], F32)
        on = A.alloc([512], BF16)
        ofT = A.alloc([4, 512], BF16)
        mrs = [A.alloc([512], F32) for _ in range(2)]

        win_v = win_d.rearrange("(c p) n -> p c n", p=128)
        for n0 in range(0, NWIN, 1880):
            T.add("pool", dma(win[:, :, n0:n0 + 1880], win_v[:, :, n0:n0 + 1880]), writes=["win%d" % n0], dma=True)
        WIN_RES = ["win%d" % n0 for n0 in range(0, NWIN, 1880)]
        T.add("sp", dma(gmt, gmt_d), writes=["gmt"], dma=True)
        for i in range(2):
            T.add("pool", mset(stgv[i], 1.0), writes=["stgv%d" % i])

        C_QR, C_QS, C_KR, C_KS, C_GT, C_QF, C_KF, C_AR, C_AF, C_VR, C_VF, C_FF = (
            0, 256, 512, 768, 1024, 1536, 2048, 2560, 3584, 4608, 5120, 5632)
        HT_RES = ["hT0", "hT1", "hT2", "hT3"]

        def norm_pre(xbuf, xres, s, gtab, gres, ring):
            k = s % ring
            T.add("act", act(junk, xbuf, AF.Square, accum=ss[:, s:s + 1]), reads=[xres], writes=["ss%d" % s])
            T.add("act", act(rs[:, s:s + 1], ss[:, s:s + 1], AF.Sqrt, bias=EPS, scale=1.0 / D),
                  reads=["ss%d" % s], writes=["rs%d" % s])
            T.add("dve", rcp(rs[:, s:s + 1], rs[:, s:s + 1]), reads=["rs%d" % s], writes=["rs%d" % s])
            T.add("dve", stt(xn[:, k, :], xbuf, rs[:, s:s + 1], gtab, ALU.mult, ALU.mult),
                  reads=[xres, "rs%d" % s, gres], writes=["xn%d" % k])

        def norm_tr(s, hbuf, hres, ring):
            k = s % ring
            b = nb()
            for c in range(8):
                T.add("pe", tr(PBb[b][:, c * 128:(c + 1) * 128], xn[:, k, c * 128:(c + 1) * 128], identb),
                      reads=["xn%d" % k, "identb"], writes=["pb%d" % b])
            T.add("act", acp(hbuf[:, :, s * 128:(s + 1) * 128], PBb[b].rearrange("p (c n) -> p c n", c=8)),
                  reads=["pb%d" % b], writes=[hres[s]])

        def norm_transpose(xbuf, xres, s, gtab, gres, hbuf, hres):
            norm_pre(xbuf, xres, s, gtab, gres, 2)
            norm_tr(s, hbuf, hres, 2)

        def proj_fm(col0, b, wt, wres, width=128):
            for c in range(8):
                T.add("pe", mm(PB[b][:, :], wt[:, c, col0:col0 + width], hT[:, c, :], c == 0, c == 7),
                      reads=wres + HT_RES, writes=["pb%d" % b])

        def load_xsub(buf, res, src, j):
            if j < NJ:
                T.add("sp", dma(buf, src[j * 128:(j + 1) * 128, :]), writes=[res], dma=True)

        stgc = [0]

        def next_stg():
            i = stgc[0] % 4
            stgc[0] += 1
            return i

        load_xsub(xb[0], "xb0", x_d, 0)
        load_xsub(xb[1], "xb1", x_d, 1)
        for t in range(NT):
            tc0 = t * 512
            hT = hTb[t % 2]
            HT_RES = HTR[t % 2]
            T.add("sp", dma(rot, rot_d[:, :, tc0:tc0 + 512].rearrange("a p n -> p a n")), writes=["rot"], dma=True)
            if t == 0:
                for s in range(4):
                    norm_pre(xb[s % 2], "xb%d" % (s % 2), s, gmt, "gmt", 4)
                    load_xsub(xb[s % 2], "xb%d" % (s % 2), x_d, s + 2)
            for s in range(4):
                norm_tr(s, hT, HT_RES, 4)
            for (cm, cs, ci, dst, dres) in ((C_QR, C_QS, 0, qrT, "qrT"), (C_KR, C_KS, 2, krT, "krT")):
                for c2 in range(2):
                    ba = nb()
                    proj_fm(cm + c2 * 128, ba, win, WIN_RES)
                    bb = nb()
                    proj_fm(cs + c2 * 128, bb, win, WIN_RES)
                    T.add("dve", tt(t1[c2], PB[ba][:, :], rot[:, ci, :], ALU.mult), reads=["pb%d" % ba, "rot"], writes=["t1%d" % c2])
                    T.add("dve", tt(t2[c2], PB[bb][:, :], rot[:, ci + 1, :], ALU.mult), reads=["pb%d" % bb, "rot"], writes=["t2%d" % c2])
                    T.add("pool", tt(dst[:, c2, :], t1[c2], t2[c2], ALU.add), reads=["t1%d" % c2, "t2%d" % c2], writes=["%s%d" % (dres, c2)])
                    if ci == 0:
                        T.add("pool", tt(qxT[:, c2, :], qrT[:, c2, :], xiT[:, c2, :], ALU.mult),
                              reads=["qrT%d" % c2, "cst"], writes=["qxT%d" % c2])
            v2 = lambda d_: d_.rearrange("(c p) s -> p c s", p=128)[:, :, tc0:tc0 + 512]
            T.add("sp", dma(v2(qr_s), qrT), reads=["qrT0", "qrT1"], dma=True)
            T.add("sp", dma(v2(kr_s), krT), reads=["krT0", "krT1"], dma=True)
            T.add("sp", dma(v2(qx_s), qxT), reads=["qxT0", "qxT1"], dma=True)
            for g in range(4):
                b = nb()
                proj_fm(C_GT + g * 128, b, win, WIN_RES)
                T.add("act", act(gtT[:, g, :], PB[b][:, :], AF.Silu), reads=["pb%d" % b], writes=["gtT"])
            T.add("sp", dma(v2(gt_s), gtT), reads=["gtT"], dma=True)
            for s in range(4):
                b = nb()
                for c in range(8):
                    T.add("pe", mm(PB[b][:, :], hT[:, c, s * 128:(s + 1) * 128], win[:, c, C_VR:C_VR + 512], c == 0, c == 7),
                          reads=WIN_RES + [HT_RES[s]], writes=["pb%d" % b])
                T.add("dve", cp(vr[:, s, :], PB[b][:, :]), reads=["pb%d" % b], writes=["vr%d" % s])
                T.add("sp", dma(vr_s[(t * 4 + s) * 128:(t * 4 + s + 1) * 128, :], vr[:, s, :]), reads=["vr%d" % s], dma=True)

            def gen_qf():
                pend = None

                def finish_qk(p):
                    (b, i, g, gcol, dstd) = p
                    b2 = nb()
                    T.add("pe", mm(PB[b2][:, :], bonesb, sqf[i], True, True), reads=["bonesb", "sqf%d" % i], writes=["pb%d" % b2])
                    T.add("act", act(rsf[i], PB[b2][:, :], AF.Sqrt, bias=EPS, scale=1.0 / 64), reads=["pb%d" % b2], writes=["rsf%d" % i])
                    T.add("dve", rcp(rsf[i], rsf[i]), reads=["rsf%d" % i], writes=["rsf%d" % i])
                    k = next_stg()
                    T.add("dve", stt(stg[k], PB[b][:, :], gcol, rsf[i], ALU.mult, ALU.mult),
                          reads=["pb%d" % b, "rsf%d" % i, "cst"], writes=["stg%d" % k])
                    T.add("sp", dma(dstd[g * 128:(g + 1) * 128, tc0:tc0 + 512], stg[k]), reads=["stg%d" % k], dma=True)

                for idx in range(8):
                    g = idx % 4
                    col = (C_QF if idx < 4 else C_KF) + g * 128
                    b = nb()
                    proj_fm(col, b, win, WIN_RES)
                    i = idx % 2
                    T.add("act", act(sqf[i], PB[b][:, :], AF.Square), reads=["pb%d" % b], writes=["sqf%d" % i])
                    if pend is not None:
                        finish_qk(pend)
                    pend = (b, i, g, gq if idx < 4 else gk, qf_s if idx < 4 else kf_s)
                    yield
                finish_qk(pend)
                yield
            def gen_af():
                for g in range(8):
                    b = nb()
                    proj_fm(C_AF + g * 128, b, win, WIN_RES)
                    k = next_stg()
                    T.add("act", act(stg[k], PB[b][:, :], AF.Sigmoid), reads=["pb%d" % b], writes=["stg%d" % k])
                    T.add("sp", dma(gaf_s[g * 128:(g + 1) * 128, tc0:tc0 + 512], stg[k]), reads=["stg%d" % k], dma=True)
                    yield
            def gen_tm():
                for s in range(4):
                    J = t * 4 + s
                    b = nb()
                    for c in range(8):
                        T.add("pe", mm(PB[b][:, :], hT[:, c, s * 128:(s + 1) * 128], win[:, c, C_VF:C_VF + 512], c == 0, c == 7),
                              reads=WIN_RES + [HT_RES[s]], writes=["pb%d" % b])
                    kv = J % 2
                    T.add("act", acp(stgv[kv][:, :, 0:64], PB[b][:, :].rearrange("p (h d) -> p h d", h=8)),
                          reads=["pb%d" % b], writes=["stgv%d" % kv])
                    T.add("sp", dma(vf_s[J * 128:(J + 1) * 128, :], stgv[kv].rearrange("p h d -> p (h d)")),
                          reads=["stgv%d" % kv], dma=True)
                    yield
                    b = nb()
                    for c in range(8):
                        T.add("pe", mm(PB[b][:, 0:8], hT[:, c, s * 128:(s + 1) * 128], win[:, c, C_FF:C_FF + 8], c == 0, c == 7),
                              reads=WIN_RES + [HT_RES[s]], writes=["pb%d" % b])
                    T.add("dve", tt(fb, PB[b][:, 0:8], bfor, ALU.add), reads=["pb%d" % b, "cst"], writes=["fb"])
                    T.add("act", act(fe, fb, AF.Exp, scale=-1.0), reads=["fb"], writes=["fe"])
                    T.add("act", act(fsp, fe, AF.Ln, bias=1.0), reads=["fe"], writes=["fsp"])
                    yield
                    b = nb()
                    T.add("pe", mm(PB[b][:, 0:8], negtri, fsp, True, True), reads=["cst", "fsp"], writes=["pb%d" % b])
                    T.add("pe", mm(PB[b][:, 8:16], negones, fsp, True, True), reads=["negones", "fsp"], writes=["pb%d" % b])
                    ca, cb = carry[J % 2], carry[(J + 1) % 2]
                    car, cbr = "carry%d" % (J % 2), "carry%d" % ((J + 1) % 2)
                    if s == 0:
                        T.add("dve", cp(mref[:, t, :], ca), reads=[car], writes=["mref"])
                    T.add("dve", tt(cK[:, J, :], PB[b][:, 0:8], ca, ALU.add), reads=["pb%d" % b, car], writes=["cK"])
                    T.add("dve", tt(cb, PB[b][:, 8:16], ca, ALU.add), reads=["pb%d" % b, car], writes=[cbr])
                    T.add("dve", tt(d8, cK[:, J, :], mref[:, t, :], ALU.subtract), reads=["cK", "mref"], writes=["d8"])
                    T.add("dve", ts(d8, d8, 8.0, ALU.mult), reads=["d8"], writes=["d8"])
                    T.add("dve", cp(hml[:, 0:8], d8), reads=["d8"], writes=["hml"])
                    T.add("dve", tt(r1, d8, hml[:, 0:8], ALU.subtract), reads=["d8", "hml"], writes=["r1"])
                    T.add("dve", cp(hml[:, 8:16], r1), reads=["r1"], writes=["hml"])
                    T.add("dve", tt(r2, r1, hml[:, 8:16], ALU.subtract), reads=["r1", "hml"], writes=["r2"])
                    T.add("dve", cp(hml[:, 16:24], r2), reads=["r2"], writes=["hml"])
                    yield
                    b = nb()
                    T.add("pe", tr(PBb[b][0:24, 0:128], hml, identb), reads=["hml", "identb"], writes=["pb%d" % b])
                    T.add("dve", cp(cqst[0:24, s * 128:(s + 1) * 128], PBb[b][0:24, 0:128]), reads=["pb%d" % b], writes=["cqst"])
                    yield
                T.add("sp", dma(cq_s[:, tc0:tc0 + 512], cqst[0:24, :]), reads=["cqst"], dma=True)
            def gen_ar():
                for g in range(8):
                    b = nb()
                    proj_fm(C_AR + g * 128, b, win, WIN_RES)
                    k = next_stg()
                    T.add("act", act(stg[k], PB[b][:, :], AF.Sigmoid), reads=["pb%d" % b], writes=["stg%d" % k])
                    T.add("sp", dma(sar_s[g * 128:(g + 1) * 128, tc0:tc0 + 512], stg[k]), reads=["stg%d" % k], dma=True)
                    yield

            def chain(*gs):
                for g_ in gs:
                    yield from g_

            def gen_norm():
                if t + 1 < NT:
                    for s in range(4):
                        norm_transpose(xb[s % 2], "xb%d" % (s % 2), s, gmt, "gmt", hTb[(t + 1) % 2], HTR[(t + 1) % 2])
                        load_xsub(xb[s % 2], "xb%d" % (s % 2), x_d, (t + 1) * 4 + s + 2)
                        yield

            gens = [chain(gen_qf(), gen_af(), gen_ar()), gen_tm()]
            alive = [True, True]
            while any(alive):
                for gi in (0, 0, 1):
                    if alive[gi]:
                        alive[gi] = next(gens[gi], 'done') != 'done'
            if t + 1 < NT:
                for s in range(4):
                    norm_pre(xb[s % 2], "xb%d" % (s % 2), s, gmt, "gmt", 4)
                    load_xsub(xb[s % 2], "xb%d" % (s % 2), x_d, (t + 1) * 4 + s + 2)

        T.barrier()
        n_ops_A = len(T.ops)
        A.off = persist_end
        KT = A.alloc([4, S], BF16)
        V = A.alloc([NJ, 520], BF16)
        qTb = [A.alloc([8, 512], BF16) for _ in range(2)]
        cqb = [A.alloc([512], BF16) for _ in range(2)]
        bias = A.alloc([NJ, 8], F32)
        pbuf = [A.alloc([512], BF16) for _ in range(6)]
        den = A.alloc([512], F32)
        bcs = A.alloc([512], F32)
        ostg = [A.alloc([512], BF16) for _ in range(2)]
        qrz = A.alloc([4, 512], BF16)
        krT = A.alloc([2, 512], BF16)
        qxT = A.alloc([2, 512], BF16)
        gtT = A.alloc([4, 512], BF16)
        vr = A.alloc([4, 512], BF16)
        kz = A.alloc([256], BF16)
        sT = A.alloc([512], BF16)
        stf = A.alloc([2, 128], F32)
        stb = A.alloc([4, 128], BF16)
        osb = A.alloc([512], F32)
        sqo = A.alloc([512], F32)
        s1 = A.alloc([4], F32)
        s2 = A.alloc([4], F32)
        mean = A.alloc([4], F32)
        msq = A.alloc([4], F32)
        var = A.alloc([4], F32)
        rstd = A.alloc([4], F32)
        mhalf = A.alloc([4], F32)
        on = A.alloc([512], BF16)
        ofT = A.alloc([4, 512], BF16)
        T.add("pool", mset(stf, 0.0), writes=["stf"])
        T.add("pool", mset(stb, 0.0), writes=["stb"])
        T.add("pool", mset(qrz, 0.0), writes=["qrze", "qrzo"])
        T.add("pool", mset(mhalf, -0.5), writes=["mhalf"])
        qrz4 = qrz.rearrange("p (c two) n -> p c two n", two=2)
        qr4 = qr_s.rearrange("(c two d) s -> d c two s", two=2, d=64)

        for i_ in range(2):
            T.add("pool", mset(qTb[i_], 0.0), writes=["qTe%d" % i_, "qTo%d" % i_])
            T.add("pool", mset(cqb[i_], 0.0), writes=["cq%d" % i_])
        qT4b = [q_.rearrange("p (c two) n -> p c two n", two=2) for q_ in qTb]

        def load_q(G_):
            if G_ < NT:
                i_ = G_ % 2
                c0_ = G_ * 512
                T.add("sp", dma(qT4b[i_][0:64, :, 0, :], qf4[:, :, 0, c0_:c0_ + 512]), writes=["qTe%d" % i_], dma=True)
                T.add("sp", dma(qT4b[i_][64:128, :, 1, :], qf4[:, :, 1, c0_:c0_ + 512]), writes=["qTo%d" % i_], dma=True)
                T.add("sp", dma(cqb[i_][0:24, :], cq_s[:, c0_:c0_ + 512]), writes=["cq%d" % i_], dma=True)

        qf4 = qf_s.rearrange("(c two d) s -> d c two s", two=2, d=64)
        KT_RES = ["KT%d" % hc for hc in range(4)]
        for hc in range(4):
            T.add("sp", dma(KT[:, hc, :], kf_s[hc * 128:(hc + 1) * 128, :]), writes=[KT_RES[hc]], dma=True)
        vf_v = vf_s.rearrange("(j p) f -> p j f", p=128)
        V_RES = []
        for j0 in range(0, NJ, 8):
            j1 = min(NJ, j0 + 8)
            T.add("sp", dma(V[:, j0:j1, :], vf_v[:, j0:j1, :]), writes=["V%d" % j0], dma=True)
            V_RES.append("V%d" % j0)

        LA = 2
        SB = [0, 1, 2]
        def gen_ret_all():
            for RG in range(NT):
                rt0 = RG * 512
                v2 = lambda d_: d_.rearrange("(c p) s -> p c s", p=128)[:, :, rt0:rt0 + 512]
                T.add("sp", dma(qrz4[0:64, :, 0, :], qr4[:, :, 0, rt0:rt0 + 512]), writes=["qrze"], dma=True)
                T.add("sp", dma(qrz4[64:128, :, 1, :], qr4[:, :, 1, rt0:rt0 + 512]), writes=["qrzo"], dma=True)
                T.add("sp", dma(krT, v2(kr_s)), writes=["krT"], dma=True)
                T.add("sp", dma(qxT, v2(qx_s)), writes=["qxT"], dma=True)
                T.add("sp", dma(gtT, v2(gt_s)), writes=["gtT"], dma=True)
                T.add("sp", dma(vr, vr_s.rearrange("(j p) f -> p j f", p=128)[:, RG * 4:RG * 4 + 4, :]), writes=["vr"], dma=True)
                yield
                for s in range(4):
                    cs_ = slice(s * 128, (s + 1) * 128)
                    for c2 in range(2):
                        T.add("pe", tr(PBb[7][:, c2 * 128:(c2 + 1) * 128], krT[:, c2, cs_], identb),
                              reads=["krT", "identb"], writes=["pb7"])
                    T.add("dve", tt(kz.rearrange("p (h d) -> p h d", h=4), PBb[7][:, 0:256].rearrange("p (h d) -> p h d", h=4),
                                    zeta.unsqueeze(2).broadcast_to([128, 4, 64]), ALU.mult),
                          reads=["pb7", "cst"], writes=["kz"])
                    for h in range(4):
                        c2 = h // 2
                        T.add("pe", mm(PB[6][:, h * 128:(h + 1) * 128], krT[:, c2, cs_], qrz[:, h, cs_], True, True),
                              reads=["krT", "qrze" if h % 2 == 0 else "qrzo"], writes=["pb6"])
                    yield
                    T.add("dve", tt(sT, PB[6][:, :], decay, ALU.mult), reads=["pb6", "cst"], writes=["sT"])
                    yield
                    for h in range(4):
                        c2 = h // 2
                        T.add("pe", mm(PB[6][:, h * 128:(h + 1) * 128], sT[:, h * 128:(h + 1) * 128], vr[:, s, h * 128:(h + 1) * 128], True, False),
                              reads=["sT", "vr"], writes=["pb6"])
                        T.add("pe", mm(PB[6][:, h * 128:(h + 1) * 128], qxT[:, c2, cs_], stb[:, h, :], False, True),
                              reads=["qxT", "stb"], writes=["pb6"])
                    for h in range(4):
                        c2, po = h // 2, (h % 2) * 64
                        T.add("pe", mm(PB[7][po:po + 64, c2 * 128:(c2 + 1) * 128], kz[:, h * 64:(h + 1) * 64], vr[:, s, h * 128:(h + 1) * 128], True, True),
                              reads=["kz", "vr"], writes=["pb7"])
                    yield
                    T.add("dve", cp(osb, PB[6][:, :]), reads=["pb6"], writes=["osb"])
                    for c2 in range(2):
                        T.add("dve", stt(stf[:, c2, :], stf[:, c2, :], gch[:, c2:c2 + 1], PB[7][:, c2 * 128:(c2 + 1) * 128], ALU.mult, ALU.add),
                              reads=["stf", "pb7", "cst"], writes=["stf"])
                    for c2 in range(2):
                        T.add("pool", cp(stb[0:64, 2 * c2, :], stf[0:64, c2, :]), reads=["stf"], writes=["stb"])
                        T.add("pool", cp(stb[64:128, 2 * c2 + 1, :], stf[64:128, c2, :]), reads=["stf"], writes=["stb"])
                    T.add("pool", tt(sqo, osb, osb, ALU.mult), reads=["osb"], writes=["sqo"])
                    T.add("dve", (lambda o, i_: (lambda e: e.tensor_reduce(out=o, in_=i_, axis=AX.X, op=ALU.add)))(
                        s1, osb.rearrange("p (h e) -> p h e", h=4)), reads=["osb"], writes=["s1"])
                    T.add("dve", ts(mean, s1, 1.0 / 128, ALU.mult), reads=["s1"], writes=["mean"])
                    T.add("dve", tt(msq, mean, mean, ALU.mult), reads=["mean"], writes=["msq"])
                    yield
                    T.add("dve", (lambda o, i_: (lambda e: e.tensor_reduce(out=o, in_=i_, axis=AX.X, op=ALU.add)))(
                        s2, sqo.rearrange("p (h e) -> p h e", h=4)), reads=["sqo"], writes=["s2"])
                    T.add("dve", stt(var, s2, 1.0 / 128, msq, ALU.mult, ALU.subtract), reads=["s2", "msq"], writes=["var"])
                    T.add("dve", ts(var, var, EPS, ALU.add), reads=["var"], writes=["var"])
                    T.add("pool", tt(rstd, var, mhalf, ALU.pow), reads=["var", "mhalf"], writes=["rstd"])
                    yield
                    for h in range(4):
                        T.add("dve", ts(on[:, h * 128:(h + 1) * 128], osb[:, h * 128:(h + 1) * 128], mean[:, h:h + 1], ALU.subtract,
                                        rstd[:, h:h + 1], ALU.mult),
                              reads=["osb", "mean", "rstd"], writes=["on"])
                    yield
                    for h in range(4):
                        T.add("pe", tr(PBb[7][:, 512 + h * 128:512 + (h + 1) * 128], on[:, h * 128:(h + 1) * 128], identb),
                              reads=["on", "identb"], writes=["pb7"])
                    for h in range(4):
                        T.add("dve", stt(ofT[:, h, cs_], PBb[7][:, 512 + h * 128:512 + (h + 1) * 128], gret[:, h:h + 1], gtT[:, h, cs_], ALU.mult, ALU.mult),
                              reads=["pb7", "gtT", "cst"], writes=["ofT"])
                    yield
                T.add("sp", dma(v2(ofr_s), ofT), reads=["ofT"], dma=True)

        g_ret = gen_ret_all()
        ret_state = {'alive': True, 'cnt': 0}
        tot_blocks = sum(8 * (4 * g_ + 4) for g_ in range(NT))
        ret_every = max(1, tot_blocks // (NT * 33 + 8))
        for G in range(NT):
            tc0 = G * 512
            nJ = 4 * G + 4
            if G == 0:
                load_q(0)
            load_q(G + 1)
            qT = qTb[G % 2]
            cq = cqb[G % 2]
            gq_ = G % 2
            T.add("dve", tt(bias[:, 0:nJ, :], mref[:, G, :].unsqueeze(1).broadcast_to([128, nJ, 8]), cK[:, 0:nJ, :], ALU.subtract),
                  reads=["mref", "cK"], writes=["bias"])
            seq = [(h, J) for h in range(8) for J in range(nJ)]
            deferred = []

            def emit_qk(i, h, J):
                hc = h // 2
                q0 = max(0, J - 4 * G) * 128
                b = SB[i % 3]
                pk = i % 6
                diag = J >= 4 * G
                T.add("pe", mm(PB[b][:, q0:512], KT[:, hc, J * 128:(J + 1) * 128], qT[:, h, q0:512], True, False),
                      reads=[KT_RES[hc], ("qTe%d" if h % 2 == 0 else "qTo%d") % gq_], writes=["pb%d" % b])
                T.add("pe", mm(PB[b][:, q0:512], eselb[:, h, :], cq[:, q0:512], False, not diag),
                      reads=["eselb", "cq%d" % gq_], writes=["pb%d" % b])
                if diag:
                    T.add("pe", mm(PB[b][:, q0:q0 + 128], identb, negmask, False, True),
                          reads=["identb", "negmask"], writes=["pb%d" % b])
                T.add("act", act(pbuf[pk][:, q0:512], PB[b][:, q0:512], AF.Exp, bias=bias[:, J, h:h + 1], scale=0.125),
                      reads=["pb%d" % b, "bias"], writes=["pbuf%d" % pk])

            def emit_pv(i, h, J):
                q0 = max(0, J - 4 * G) * 128
                pk = i % 6
                ob = 3 + h % 2
                T.add("pe", mm(PB[ob][0:65, q0:512], V[:, J, h * 65:(h + 1) * 65], pbuf[pk][:, q0:512], J == 0, J == nJ - 1),
                      reads=["V%d" % (J // 8 * 8), "pbuf%d" % pk], writes=["pb%d" % ob])

            def epi1(h):
                ob = 3 + h % 2
                T.add("dve", cp(den[64:65, :], PB[ob][64:65, :]), reads=["pb%d" % ob], writes=["den"])
                T.add("dve", rcp(den[64:65, :], den[64:65, :]), reads=["den"], writes=["den"])

            def epi2(h):
                ob = 3 + h % 2
                k = h % 2
                T.add("pe", mm(PB[5][0:64, :], onesf[64:65, 0:64], den[64:65, :], True, True), reads=["onesf", "den"], writes=["pb5"])
                T.add("dve", cp(bcs[0:64, :], PB[5][0:64, :]), reads=["pb5"], writes=["bcs"])
                T.add("dve", tt(ostg[k][0:64, :], PB[ob][0:64, :], bcs[0:64, :], ALU.mult), reads=["pb%d" % ob, "bcs"], writes=["ostg%d" % k])
                T.add("sp", dma(of_s[h * 64:(h + 1) * 64, tc0:tc0 + 512], ostg[k][0:64, :]), reads=["ostg%d" % k], dma=True)

            n = len(seq)
            for i in range(n + LA + 3):
                if i < n:
                    ret_state['cnt'] += 1
                    if ret_state['alive'] and ret_state['cnt'] % ret_every == 0:
                        ret_state['alive'] = next(g_ret, 'done') != 'done'
                    emit_qk(i, *seq[i])
                j = i - LA
                if 0 <= j < n:
                    h, J = seq[j]
                    emit_pv(j, h, J)
                    if J == nJ - 1:
                        epi1(h)
                        deferred.append((i + 2, h))
                while deferred and deferred[0][0] <= i:
                    epi2(deferred.pop(0)[1])
            assert not deferred

        while ret_state['alive']:
            ret_state['alive'] = next(g_ret, 'done') != 'done'

        T.barrier()
        n_ops_B = len(T.ops)
        A.off = phaseC_start
        wg = A.alloc([8, DFF], BF16)
        wu = A.alloc([8, DFF], BF16)
        b2_start = A.off
        wfo = A.alloc([4, D], BF16)
        wro = A.alloc([4, D], BF16)
        wout = A.alloc([8, D], BF16)
        on8 = [A.alloc([4, 512], BF16) for _ in range(2)]
        ofr = [A.alloc([4, 512], BF16) for _ in range(2)]
        gaf = [A.alloc([8, 512], BF16) for _ in range(2)]
        sar = [A.alloc([8, 512], BF16) for _ in range(2)]
        tmpf = [A.alloc([512], F32) for _ in range(2)]
        tmpr = [A.alloc([512], F32) for _ in range(2)]
        mgT = A.alloc([8, 512], BF16)
        xb2 = [A.alloc([D], F32) for _ in range(2)]
        b2_end = A.off
        T.add("pool", dma(wfo, wfo_d.rearrange("(c p) n -> p c n", p=128)), writes=["wfo"], dma=True)
        T.add("pool", dma(wro, wro_d.rearrange("(c p) n -> p c n", p=128)), writes=["wro"], dma=True)
        T.add("pool", dma(wout, wout_d.rearrange("(c p) n -> p c n", p=128)), writes=["wout"], dma=True)
        wg_v = wg_d.rearrange("(c p) n -> p c n", p=128)
        wu_v = wu_d.rearrange("(c p) n -> p c n", p=128)
        WG_RES, WU_RES = [], []
        for n0 in range(0, DFF, 1408):
            T.add("pool", dma(wg[:, :, n0:n0 + 1408], wg_v[:, :, n0:n0 + 1408]), writes=["wg%d" % n0], dma=True)
            T.add("pool", dma(wu[:, :, n0:n0 + 1408], wu_v[:, :, n0:n0 + 1408]), writes=["wu%d" % n0], dma=True)
            WG_RES.append("wg%d" % n0)
            WU_RES.append("wu%d" % n0)
        def load_b2(G):
            if G < NT:
                c0_ = G * 512
                T.add("sp", dma(on8[G % 2], of_s.rearrange("(c p) s -> p c s", p=128)[:, :, c0_:c0_ + 512]), writes=["on8%d" % (G % 2)], dma=True)
                T.add("sp", dma(gaf[G % 2], gaf_s.rearrange("(c p) s -> p c s", p=128)[:, :, c0_:c0_ + 512]), writes=["gaf%d" % (G % 2)], dma=True)
                T.add("sp", dma(ofr[G % 2], ofr_s.rearrange("(c p) s -> p c s", p=128)[:, :, c0_:c0_ + 512]), writes=["ofr%d" % (G % 2)], dma=True)
                T.add("sp", dma(sar[G % 2], sar_s.rearrange("(c p) s -> p c s", p=128)[:, :, c0_:c0_ + 512]), writes=["sar%d" % (G % 2)], dma=True)

        load_b2(0)
        for G in range(NT):
            tc0 = G * 512
            load_b2(G + 1)
            for dg in range(8):
                b = (2 * dg) % 8
                b2_ = (2 * dg + 1) % 8
                k = dg % 2
                for c in range(4):
                    T.add("pe", mm(PB[b][:, :], wfo[:, c, dg * 128:(dg + 1) * 128], on8[G % 2][:, c, :], c == 0, c == 3),
                          reads=["wfo", "on8%d" % (G % 2)], writes=["pb%d" % b])
                for c in range(4):
                    T.add("pe", mm(PB[b2_][:, :], wro[:, c, dg * 128:(dg + 1) * 128], ofr[G % 2][:, c, :], c == 0, c == 3),
                          reads=["wro", "ofr%d" % (G % 2)], writes=["pb%d" % b2_])
                T.add("dve", tt(tmpf[k], PB[b][:, :], gaf[G % 2][:, dg, :], ALU.mult), reads=["pb%d" % b, "gaf%d" % (G % 2)], writes=["tmpf%d" % k])
                T.add("dve", tt(tmpr[k], PB[b2_][:, :], sar[G % 2][:, dg, :], ALU.mult), reads=["pb%d" % b2_, "sar%d" % (G % 2)], writes=["tmpr%d" % k])
                T.add("pool", tt(mgT[:, dg, :], tmpf[k], tmpr[k], ALU.add), reads=["tmpf%d" % k, "tmpr%d" % k], writes=["mgT"])
            for s in range(4):
                load_xsub(xb2[s % 2], "xb2%d" % (s % 2), x_d, G * 4 + s)
                for half in range(2):
                    b = (2 * s + half) % 8
                    for c in range(8):
                        T.add("pe", mm(PB[b][:, :], mgT[:, c, s * 128:(s + 1) * 128], wout[:, c, half * 512:(half + 1) * 512], c == 0, c == 7),
                              reads=["mgT", "wout"], writes=["pb%d" % b])
                    T.add("dve", tt(xb2[s % 2][:, half * 512:(half + 1) * 512], PB[b][:, :], xb2[s % 2][:, half * 512:(half + 1) * 512], ALU.add),
                          reads=["pb%d" % b, "xb2%d" % (s % 2)], writes=["xb2%d" % (s % 2)])
                r0 = (G * 4 + s) * 128
                T.add("sp", dma(out_d[r0:r0 + 128, :], xb2[s % 2]), reads=["xb2%d" % (s % 2)], dma=True)

        T.barrier()
        n_ops_B2 = len(T.ops)
        A.off = b2_start
        wd = A.alloc([22, D], BF16)
        wd_v = wd_d.rearrange("(c p) n -> p c n", p=128)
        WD_RES = []
        for c0 in range(0, 22, 11):
            T.add("pool", dma(wd[:, c0:c0 + 11, :], wd_v[:, c0:c0 + 11, :]), writes=["wd%d" % c0], dma=True)
            WD_RES.append("wd%d" % c0)
        gft = A.alloc([D], F32)
        xc = [A.alloc([D], F32) for _ in range(2)]
        xr = [A.alloc([D], F32) for _ in range(2)]
        junk = A.alloc([D], BF16)
        ss = A.alloc([4], F32)
        rs = A.alloc([4], F32)
        xn = A.alloc([2, D], BF16)
        hTb = [A.alloc([8, 512], BF16) for _ in range(2)]
        sg = [A.alloc([512], F32) for _ in range(2)]
        hid = A.alloc([22, 512], BF16)

        pass
        T.add("sp", dma(gft, gft_d), writes=["gft"], dma=True)

        pbc[0] = 0
        load_xsub(xc[0], "xc0", out_d, 0)
        load_xsub(xc[1], "xc1", out_d, 1)
        for s in range(4):
            norm_transpose(xc[s % 2], "xc%d" % (s % 2), s, gft, "gft", hTb[0], HTR[0])
            load_xsub(xc[s % 2], "xc%d" % (s % 2), out_d, s + 2)
        for t in range(NT):
            hT = hTb[t % 2]
            HT_RES = HTR[t % 2]
            for fc in range(22):
                if t + 1 < NT and fc in (0, 4, 8, 12):
                    s = (0, 4, 8, 12).index(fc)
                    norm_pre(xc[s % 2], "xc%d" % (s % 2), s, gft, "gft", 2)
                    load_xsub(xc[s % 2], "xc%d" % (s % 2), out_d, (t + 1) * 4 + s + 2)
                if t + 1 < NT and fc in (3, 7, 11, 15):
                    s = (3, 7, 11, 15).index(fc)
                    norm_tr(s, hTb[(t + 1) % 2], HTR[(t + 1) % 2], 2)
                ba = nb(6)
                proj_fm(fc * 128, ba, wg, WG_RES)
                bb = nb(6)
                proj_fm(fc * 128, bb, wu, WU_RES)
                k = fc % 2
                T.add("act", act(sg[k], PB[ba][:, :], AF.Silu), reads=["pb%d" % ba], writes=["sg%d" % k])
                T.add("dve", tt(hid[:, fc, :], PB[bb][:, :], sg[k], ALU.mult), reads=["pb%d" % bb, "sg%d" % k], writes=["hid"])
            for s in range(4):
                k = s % 2
                load_xsub(xr[k], "xr%d" % k, out_d, t * 4 + s)
                for half in range(2):
                    b = 6 + half
                    for fc in range(22):
                        T.add("pe", mm(PB[b][:, :], hid[:, fc, s * 128:(s + 1) * 128], wd[:, fc, half * 512:(half + 1) * 512], fc == 0, fc == 21),
                              reads=["hid"] + WD_RES, writes=["pb%d" % b])
                    T.add("dve", tt(xr[k][:, half * 512:(half + 1) * 512], PB[b][:, :], xr[k][:, half * 512:(half + 1) * 512], ALU.add),
                          reads=["pb%d" % b, "xr%d" % k], writes=["xr%d" % k])
                r0 = (t * 4 + s) * 128
                T.add("sp", dma(out_d[r0:r0 + 128, :], xr[k]), reads=["xr%d" % k], dma=True)

        def sem_alloc(name):
            return st.enter_context(nc.semaphore(name))

        _stop = os.environ.get("K_STOP", "")
        if _stop:
            ncut = {"A": n_ops_A, "B": n_ops_B, "B2": n_ops_B2}.get(_stop, int(_stop) if _stop.isdigit() else len(T.ops))
            del T.ops[ncut:]
            T.dma_count = {}
            for op in T.ops:
                if op["dma"]:
                    T.dma_count[op["key"]] = T.dma_count.get(op["key"], 0) + 1
        run = T.prepare(sem_alloc)
        with nc.Block() as block:
            block.sync(lambda e: run("sp", e))
            block.tensor(lambda e: run("pe", e))
            block.scalar(lambda e: run("act", e))
            block.vector(lambda e: run("dve", e))
            block.gpsimd(lambda e: run("pool", e))
    nc._dbg_iname = T.iname
    return nc


def host_constants(S):
    f64 = np.float64
    pos = np.arange(S, dtype=np.float32)
    half = 32
    inv_freq = (1.0 / (np.float32(10000.0) ** (np.arange(half, dtype=np.float32) / np.float32(half)))).astype(np.float32)
    p = np.arange(128)
    d = p % 64
    ang = (pos[None, :] * inv_freq[d % 32][:, None]).astype(np.float32).astype(f64)
    cos = np.cos(ang)
    sin = np.sin(ang) * np.where(d < 32, -1.0, 1.0)[:, None]
    rot = np.stack([cos, sin, 0.125 * cos, 0.125 * sin]).astype(np.float32)
    log_g = np.log1p(-(2.0 ** (-5.0 - np.arange(4, dtype=f64))))
    i = np.arange(128)
    diff = i[None, :] - i[:, None]
    decay = np.zeros((128, 4, 128), f64)
    for h in range(4):
        decay[:, h, :] = np.where(diff >= 0, np.exp(np.maximum(diff, 0) * log_g[h]), 0.0)
    xi = np.zeros((128, 2, 512), f64)
    for c2 in range(2):
        hd = 2 * c2 + p // 64
        xi[:, c2, :] = np.exp((np.arange(512) % 128 + 1.0)[None, :] * log_g[hd][:, None])
    zeta = np.exp((127.0 - i)[:, None] * log_g[None, :])
    gch = np.stack([np.exp(128.0 * log_g[2 * c2 + p // 64]) for c2 in range(2)], axis=1)
    ident = np.eye(128)
    negtri = np.where(i[:, None] <= i[None, :], -1.0, 0.0)
    bones = np.kron(np.eye(2), np.ones((64, 64)))
    esel = np.zeros((24, 8, 128), np.float32)
    for h in range(8):
        for k in range(3):
            esel[8 * k + h, h, :] = 1.0
    return rot, ident, negtri, bones, decay.reshape(128, 512), xi.reshape(128, 1024), zeta, gch, esel.reshape(24, 1024)


def win_perm():
    def swap(base):
        idx = []
        for h in range(4):
            idx += list(range(base + h * 64 + 32, base + h * 64 + 64)) + list(range(base + h * 64, base + h * 64 + 32))
        return idx
    r = lambda a, n: list(range(a, a + n))
    perm = (r(0, 256) + swap(0) + r(256, 256) + swap(256) + r(1024, 512) + r(1536, 512) + r(2048, 512)
            + r(3080, 1024) + r(4104, 1024) + r(512, 512) + r(2560, 512) + r(3072, 8))
    assert len(perm) == NWIN
    return np.array(perm)


def make_in_maps(S, x, g_mix, w_in, b_forget, g_ret_norm, w_ret_o, g_fox_q, g_fox_k, w_fox_o,
                 w_out, g_ffn, w_gate, w_up, w_down):
    f = lambda a: np.ascontiguousarray(np.asarray(a, dtype=np.float32))
    rot, ident, negtri, bones, decay, xi, zeta, gch, esel = host_constants(S)
    cst = np.zeros((128, NCST), np.float32)
    cst[:, 0:128] = ident
    cst[:, 128:256] = negtri
    cst[:, 256:384] = bones
    cst[:, 384:896] = decay
    cst[:, 896:1920] = xi
    cst[:, 1920:1924] = f(g_ret_norm)[0].reshape(4, 128).T
    cst[:, 1924] = np.tile(f(g_fox_q)[0], 2)
    cst[:, 1925] = np.tile(f(g_fox_k)[0], 2)
    cst[:, 1926:1928] = gch
    cst[:, 1928:1932] = zeta
    cst[:, 1932:1940] = np.broadcast_to(f(b_forget)[0][None, :], (128, 8))
    shared = {
        "win": f(f(w_in)[0][:, win_perm()]),
        "wro": f(w_ret_o)[0], "wfo": f(w_fox_o)[0], "wout": f(w_out)[0],
        "wg": f(w_gate)[0], "wu": f(w_up)[0], "wd": f(w_down)[0],
        "gmt": f(np.broadcast_to(f(g_mix)[0][None, :], (128, D))),
        "gft": f(np.broadcast_to(f(g_ffn)[0][None, :], (128, D))),
        "cst": cst, "esel": f(esel), "rot": f(rot),
    }
    xs = f(x)
    return [dict(shared, x=np.ascontiguousarray(xs[b])) for b in range(xs.shape[0])]


_NC_CACHE = {}


def kernel(**inputs):
    x = np.asarray(inputs["x"])
    B, S, _ = x.shape
    if S not in _NC_CACHE:
        _NC_CACHE[S] = build_nc(S)
    nc = _NC_CACHE[S]
    in_maps = make_in_maps(S, **inputs)
    res = run_bass_kernel_spmd(nc, in_maps, core_ids=list(range(B)))
    return np.stack([np.asarray(r["out"]) for r in res.results], axis=0).astype(np.float32)
```

```python
import contextlib
import os
import sys
import numpy as np
import concourse.bass as bass
import concourse.mybir as mybir
from concourse.bass_utils import run_bass_kernel_spmd

F32 = mybir.dt.float32
BF16 = mybir.dt.bfloat16
AF = mybir.ActivationFunctionType
ALU = mybir.AluOpType
AX = mybir.AxisListType

D = 1024
DFF = 2816
EPS = 1e-6
NWIN = 5640
NCST = 1940


class Tracker:
    ENGS = ("pe", "act", "dve", "pool", "sp")

    def __init__(self):
        self.ops = []
        self.last_writer = {}
        self.readers = {}
        self.last_on_eng = {}
        self.dma_last = {}
        self.dma_count = {}
        self.barrier_deps = {e: set() for e in self.ENGS}
        self.iname = {}

    def add(self, eng, emit, reads=(), writes=(), dma=False):
        idx = len(self.ops)
        deps = set(self.barrier_deps[eng])
        self.barrier_deps[eng] = set()
        for r in reads:
            w = self.last_writer.get(r)
            if w is not None:
                deps.add(w)
        for wr in writes:
            w = self.last_writer.get(wr)
            if w is not None:
                deps.add(w)
            for rd in self.readers.get(wr, ()):
                deps.add(rd)
        key = None
        if dma:
            if len(writes) == 0 or str(writes[0]).startswith("hbm"):
                key = ("st", reads[0])
            else:
                key = ("ld", writes[0])
            self.dma_count[key] = self.dma_count.get(key, 0) + 1
        op = dict(eng=eng, emit=emit, deps=deps, dma=dma, key=key, line=sys._getframe(1).f_lineno,
                  kval=(16 * self.dma_count[key] if dma else None), sig=False, sval=None)
        self.ops.append(op)
        for r in reads:
            lst = self.readers.setdefault(r, [])
            if not dma:
                lst[:] = [i for i in lst if self.ops[i]["dma"] or self.ops[i]["eng"] != eng]
            lst.append(idx)
        for wr in writes:
            self.last_writer[wr] = idx
            self.readers[wr] = []
        self.last_on_eng[eng] = idx
        if dma:
            self.dma_last[key] = idx
        return idx

    def barrier(self):
        s = set(self.last_on_eng.values()) | set(self.dma_last.values())
        for e in self.ENGS:
            self.barrier_deps[e] |= s

    def prepare(self, sem_alloc):
        ops = self.ops
        for op in ops:
            nd = set()
            for d in op["deps"]:
                dop = ops[d]
                if (not dop["dma"]) and (not op["dma"]) and dop["eng"] == op["eng"] == "pe":
                    continue
                nd.add(d)
                if not dop["dma"]:
                    dop["sig"] = True
            op["deps"] = nd
        cnt = {e: 0 for e in self.ENGS}
        for op in ops:
            if op["dma"]:
                continue
            if op["sig"]:
                cnt[op["eng"]] += 1
                op["sval"] = cnt[op["eng"]]
        esem = {e: sem_alloc("sem_" + e) for e in self.ENGS}
        ksem = {}
        for op in ops:
            if op["dma"] and op["key"] not in ksem:
                ksem[op["key"]] = sem_alloc("dk%d" % len(ksem))
        per_eng = {e: [] for e in self.ENGS}
        for i, op in enumerate(ops):
            per_eng[op["eng"]].append(i)

        def run_engine(ename, eng):
            seen = {}
            for i in per_eng[ename]:
                op = ops[i]
                for d in sorted(op["deps"]):
                    dop = ops[d]
                    if dop["dma"]:
                        sem, val, sid = ksem[dop["key"]], dop["kval"], ("k", dop["key"])
                    else:
                        sem, val, sid = esem[dop["eng"]], dop["sval"], ("e", dop["eng"])
                    if seen.get(sid, 0) >= val:
                        continue
                    eng.wait_ge(sem, val)
                    seen[sid] = val
                ins = op["emit"](eng)
                try:
                    self.iname[ins.ins.name] = op["line"]
                except Exception:
                    pass
                if op["dma"]:
                    ins.then_inc(ksem[op["key"]], 16)
                elif op["sig"]:
                    ins.then_inc(esem[ename], 1)
            if ename == "sp":
                for key, n in self.dma_count.items():
                    eng.wait_ge(ksem[key], 16 * n)

        return run_engine


class Arena:
    def __init__(self, big, nbytes):
        self.big = big
        self.n = nbytes
        self.off = 0

    def alloc(self, shape, dtype):
        esz = 4 if dtype == F32 else 2
        nel = int(np.prod(shape))
        nb = (nel * esz + 63) // 64 * 64
        assert self.off + nb <= self.n, ("arena overflow", self.off, nb, self.n)
        a = self.off // 2
        b = a + nel * esz // 2
        self.off += nb
        v = self.big[:, a:b]
        if dtype == F32:
            v = v.bitcast(F32)
        if len(shape) == 2:
            v = v.rearrange("p (a b) -> p a b", a=shape[0])
        elif len(shape) == 3:
            v = v.rearrange("p (a b c) -> p a b c", a=shape[0], b=shape[1])
        return v


def mm(out, lhsT, rhs, start, stop):
    return lambda e: e.matmul(out, lhsT=lhsT, rhs=rhs, start=start, stop=stop)


def tr(out, in_, ident):
    return lambda e: e.transpose(out=out, in_=in_, identity=ident)


def act(out, in_, func, bias=None, scale=None, accum=None):
    kw = {}
    if bias is not None:
        kw["bias"] = bias
    if scale is not None:
        kw["scale"] = scale
    if accum is not None:
        kw["accum_out"] = accum
    return lambda e: e.activation(out=out, in_=in_, func=func, **kw)


def tt(out, a, b, op):
    return lambda e: e.tensor_tensor(out=out, in0=a, in1=b, op=op)


def ts(out, a, s1, op0, s2=None, op1=None):
    if op1 is None:
        return lambda e: e.tensor_scalar(out=out, in0=a, scalar1=s1, scalar2=None, op0=op0)
    return lambda e: e.tensor_scalar(out=out, in0=a, scalar1=s1, scalar2=s2, op0=op0, op1=op1)


def stt(out, a, s, b, op0, op1):
    return lambda e: e.scalar_tensor_tensor(out=out, in0=a, scalar=s, in1=b, op0=op0, op1=op1)


def cp(out, in_):
    return lambda e: e.tensor_copy(out=out, in_=in_)


def acp(out, in_):
    return lambda e: e.copy(out=out, in_=in_)


def rcp(out, in_):
    return lambda e: e.reciprocal(out=out, in_=in_)


def dma(out, in_):
    return lambda e: e.dma_start(out=out, in_=in_)


def mset(ap, v):
    return lambda e: e.memset(ap, v)


def build_nc(S):
    NT = S // 512
    NJ = S // 128
    nc = bass.Bass("TRN2", target_bir_lowering=False)

    def din(name, shape, dt=F32):
        return nc.dram_tensor(name, shape, dt, kind="ExternalInput").ap()

    x_d = din("x", [S, D])
    win_d = din("win", [D, NWIN])
    wro_d = din("wro", [512, D])
    wfo_d = din("wfo", [512, D])
    wout_d = din("wout", [D, D])
    wg_d = din("wg", [D, DFF])
    wu_d = din("wu", [D, DFF])
    wd_d = din("wd", [DFF, D])
    gmt_d = din("gmt", [128, D])
    gft_d = din("gft", [128, D])
    cst_d = din("cst", [128, NCST])
    esel_d = din("esel", [24, 1024])
    rot_d = din("rot", [4, 128, S])
    out_d = nc.dram_tensor("out", [S, D], F32, kind="ExternalOutput").ap()
    qf_s = nc.dram_tensor("qf_s", [512, S], BF16, kind="Internal").ap()
    kf_s = nc.dram_tensor("kf_s", [512, S], BF16, kind="Internal").ap()
    vf_s = nc.dram_tensor("vf_s", [S, 520], BF16, kind="Internal").ap()
    gaf_s = nc.dram_tensor("gaf_s", [D, S], BF16, kind="Internal").ap()
    qr_s = nc.dram_tensor("qr_s", [256, S], BF16, kind="Internal").ap()
    kr_s = nc.dram_tensor("kr_s", [256, S], BF16, kind="Internal").ap()
    qx_s = nc.dram_tensor("qx_s", [256, S], BF16, kind="Internal").ap()
    gt_s = nc.dram_tensor("gt_s", [512, S], BF16, kind="Internal").ap()
    vr_s = nc.dram_tensor("vr_s", [S, 512], BF16, kind="Internal").ap()
    sar_s = nc.dram_tensor("sar_s", [D, S], BF16, kind="Internal").ap()
    ofr_s = nc.dram_tensor("ofr_s", [512, S], BF16, kind="Internal").ap()
    cq_s = nc.dram_tensor("cq_s", [24, S], BF16, kind="Internal").ap()
    of_s = nc.dram_tensor("of_s", [512, S], BF16, kind="Internal").ap()

    T = Tracker()
    ARENA_BYTES = 207 * 1024
    with contextlib.ExitStack() as st:
        big = st.enter_context(nc.sbuf_tensor("big", [128, ARENA_BYTES // 2], BF16))
        PB = [st.enter_context(nc.psum_tensor("pb%d" % i, [128, 512], F32)) for i in range(8)]
        PBb = [p[:, :].bitcast(BF16) for p in PB]
        A = Arena(big, ARENA_BYTES)

        identb = A.alloc([128], BF16)
        bonesb = A.alloc([128], BF16)
        phaseC_start = A.off
        cst = A.alloc([NCST], F32)
        identf = cst[:, 0:128]
        negtri = cst[:, 128:256]
        bonesf = cst[:, 256:384]
        decay = cst[:, 384:896]
        xiT = cst[:, 896:1920].rearrange("p (c n) -> p c n", c=2)
        gret = cst[:, 1920:1924]
        gq = cst[:, 1924:1925]
        gk = cst[:, 1925:1926]
        gch = cst[:, 1926:1928]
        zeta = cst[:, 1928:1932]
        bfor = cst[:, 1932:1940]
        eselb = A.alloc([8, 128], BF16)
        negmask = A.alloc([128], BF16)
        negones = A.alloc([128], F32)
        onesf = A.alloc([128], F32)
        cK = A.alloc([NJ, 8], F32)
        mref = A.alloc([NT, 8], F32)
        carry = [A.alloc([8], F32) for _ in range(2)]
        persist_end = A.off
        eself = A.alloc([1024], F32)

        T.add("sp", dma(cst, cst_d), writes=["cst"], dma=True)
        T.add("pool", mset(eself, 0.0), writes=["eself"])
        T.add("sp", dma(eself[0:24, :], esel_d), writes=["eself"], dma=True)
        T.add("dve", cp(identb, identf), reads=["cst"], writes=["identb"])
        T.add("dve", cp(bonesb, bonesf), reads=["cst"], writes=["bonesb"])
        T.add("dve", cp(eselb.rearrange("p a b -> p (a b)"), eself), reads=["eself"], writes=["eselb"])
        T.add("dve", ts(negmask, negtri, 1.0, ALU.add, -30000.0, ALU.mult), reads=["cst"], writes=["negmask"])
        T.add("pool", mset(negones, -1.0), writes=["negones"])
        T.add("pool", mset(onesf, 1.0), writes=["onesf"])
        T.add("pool", mset(carry[0], 0.0), writes=["carry0"])

        pbc = [0]

        def nb(n=8, base=0):
            b = base + pbc[0] % n
            pbc[0] += 1
            return b

        win = A.alloc([8, NWIN], BF16)
        gmt = A.alloc([D], F32)
        xb = [A.alloc([D], F32) for _ in range(2)]
        junk = A.alloc([D], BF16)
        ss = A.alloc([4], F32)
        rs = A.alloc([4], F32)
        xn = A.alloc([4, D], BF16)
        hTb = [A.alloc([8, 512], BF16) for _ in range(2)]
        HTR = [["hT0", "hT1", "hT2", "hT3"], ["hU0", "hU1", "hU2", "hU3"]]
        hT = hTb[0]
        rot = A.alloc([4, 512], F32)
        t1 = [A.alloc([512], F32) for _ in range(2)]
        t2 = [A.alloc([512], F32) for _ in range(2)]
        qrT = A.alloc([2, 512], BF16)
        krT = A.alloc([2, 512], BF16)
        qxT = A.alloc([2, 512], BF16)
        qrz = A.alloc([4, 512], BF16)
        gtT = A.alloc([4, 512], BF16)
        sqf = [A.alloc([512], BF16) for _ in range(2)]
        rsf = [A.alloc([512], F32) for _ in range(2)]
        stg = [A.alloc([512], BF16) for _ in range(4)]
        stgv = [A.alloc([8, 65], BF16) for _ in range(2)]
        vr = A.alloc([4, 512], BF16)
        sgt = [A.alloc([512], BF16) for _ in range(2)]
        fb = A.alloc([8], F32)
        fe = A.alloc([8], F32)
        fsp = A.alloc([8], F32)
        d8 = A.alloc([8], F32)
        r1 = A.alloc([8], F32)
        r2 = A.alloc([8], F32)
        hml = A.alloc([24], BF16)
        cqst = A.alloc([512], BF16)
        kz = A.alloc([256], BF16)
        sT = A.alloc([512], BF16)
        stf = A.alloc([2, 128], F32)
        stb = A.alloc([4, 128], BF16)
        s1 = A.alloc([4], F32)
        s2 = A.alloc([4], F32)
        mean = A.alloc([4], F32)
        msq = A.alloc([4], F32)
        var = A.alloc([4], F32)
        rstd = A.alloc([4], F32)
        nbias = A.alloc([4], F32)
        on = A.alloc([512], BF16)
        ofT = A.alloc([4, 512], BF16)
        mrs = [A.alloc([512], F32) for _ in range(2)]

        win_v = win_d.rearrange("(c p) n -> p c n", p=128)
        for n0 in range(0, NWIN, 1880):
            T.add("pool", dma(win[:, :, n0:n0 + 1880], win_v[:, :, n0:n0 + 1880]), writes=["win%d" % n0], dma=True)
        WIN_RES = ["win%d" % n0 for n0 in range(0, NWIN, 1880)]
        T.add("sp", dma(gmt, gmt_d), writes=["gmt"], dma=True)
        for i in range(2):
            T.add("pool", mset(stgv[i], 1.0), writes=["stgv%d" % i])

        C_QR, C_QS, C_KR, C_KS, C_GT, C_QF, C_KF, C_AR, C_AF, C_VR, C_VF, C_FF = (
            0, 256, 512, 768, 1024, 1536, 2048, 2560, 3584, 4608, 5120, 5632)
        HT_RES = ["hT0", "hT1", "hT2", "hT3"]

        def norm_pre(xbuf, xres, s, gtab, gres, ring):
            k = s % ring
            T.add("act", act(junk, xbuf, AF.Square, accum=ss[:, s:s + 1]), reads=[xres], writes=["ss%d" % s])
            T.add("act", act(rs[:, s:s + 1], ss[:, s:s + 1], AF.Sqrt, bias=EPS, scale=1.0 / D),
                  reads=["ss%d" % s], writes=["rs%d" % s])
            T.add("dve", rcp(rs[:, s:s + 1], rs[:, s:s + 1]), reads=["rs%d" % s], writes=["rs%d" % s])
            T.add("dve", stt(xn[:, k, :], xbuf, rs[:, s:s + 1], gtab, ALU.mult, ALU.mult),
                  reads=[xres, "rs%d" % s, gres], writes=["xn%d" % k])

        def norm_tr(s, hbuf, hres, ring):
            k = s % ring
            b = nb()
            for c in range(8):
                T.add("pe", tr(PBb[b][:, c * 128:(c + 1) * 128], xn[:, k, c * 128:(c + 1) * 128], identb),
                      reads=["xn%d" % k, "identb"], writes=["pb%d" % b])
            T.add("act", acp(hbuf[:, :, s * 128:(s + 1) * 128], PBb[b].rearrange("p (c n) -> p c n", c=8)),
                  reads=["pb%d" % b], writes=[hres[s]])

        def norm_transpose(xbuf, xres, s, gtab, gres, hbuf, hres):
            norm_pre(xbuf, xres, s, gtab, gres, 2)
            norm_tr(s, hbuf, hres, 2)

        def proj_fm(col0, b, wt, wres, width=128):
            for c in range(8):
                T.add("pe", mm(PB[b][:, :], wt[:, c, col0:col0 + width], hT[:, c, :], c == 0, c == 7),
                      reads=wres + HT_RES, writes=["pb%d" % b])

        def load_xsub(buf, res, src, j):
            if j < NJ:
                T.add("sp", dma(buf, src[j * 128:(j + 1) * 128, :]), writes=[res], dma=True)

        stgc = [0]

        def next_stg():
            i = stgc[0] % 4
            stgc[0] += 1
            return i

        load_xsub(xb[0], "xb0", x_d, 0)
        load_xsub(xb[1], "xb1", x_d, 1)
        for t in range(NT):
            tc0 = t * 512
            hT = hTb[t % 2]
            HT_RES = HTR[t % 2]
            T.add("sp", dma(rot, rot_d[:, :, tc0:tc0 + 512].rearrange("a p n -> p a n")), writes=["rot"], dma=True)
            if t == 0:
                for s in range(4):
                    norm_pre(xb[s % 2], "xb%d" % (s % 2), s, gmt, "gmt", 4)
                    load_xsub(xb[s % 2], "xb%d" % (s % 2), x_d, s + 2)
            for s in range(4):
                norm_tr(s, hT, HT_RES, 4)
            for (cm, cs, ci, dst, dres) in ((C_QR, C_QS, 0, qrT, "qrT"), (C_KR, C_KS, 2, krT, "krT")):
                for c2 in range(2):
                    ba = nb()
                    proj_fm(cm + c2 * 128, ba, win, WIN_RES)
                    bb = nb()
                    proj_fm(cs + c2 * 128, bb, win, WIN_RES)
                    T.add("dve", tt(t1[c2], PB[ba][:, :], rot[:, ci, :], ALU.mult), reads=["pb%d" % ba, "rot"], writes=["t1%d" % c2])
                    T.add("dve", tt(t2[c2], PB[bb][:, :], rot[:, ci + 1, :], ALU.mult), reads=["pb%d" % bb, "rot"], writes=["t2%d" % c2])
                    T.add("pool", tt(dst[:, c2, :], t1[c2], t2[c2], ALU.add), reads=["t1%d" % c2, "t2%d" % c2], writes=["%s%d" % (dres, c2)])
                    if ci == 0:
                        T.add("pool", tt(qxT[:, c2, :], qrT[:, c2, :], xiT[:, c2, :], ALU.mult),
                              reads=["qrT%d" % c2, "cst"], writes=["qxT%d" % c2])
            v2 = lambda d_: d_.rearrange("(c p) s -> p c s", p=128)[:, :, tc0:tc0 + 512]
            T.add("sp", dma(v2(qr_s), qrT), reads=["qrT0", "qrT1"], dma=True)
            T.add("sp", dma(v2(kr_s), krT), reads=["krT0", "krT1"], dma=True)
            T.add("sp", dma(v2(qx_s), qxT), reads=["qxT0", "qxT1"], dma=True)
            for g in range(4):
                b = nb()
                proj_fm(C_GT + g * 128, b, win, WIN_RES)
                T.add("act", act(gtT[:, g, :], PB[b][:, :], AF.Silu), reads=["pb%d" % b], writes=["gtT"])
            T.add("sp", dma(v2(gt_s), gtT), reads=["gtT"], dma=True)
            for s in range(4):
                b = nb()
                for c in range(8):
                    T.add("pe", mm(PB[b][:, :], hT[:, c, s * 128:(s + 1) * 128], win[:, c, C_VR:C_VR + 512], c == 0, c == 7),
                          reads=WIN_RES + [HT_RES[s]], writes=["pb%d" % b])
                T.add("dve", cp(vr[:, s, :], PB[b][:, :]), reads=["pb%d" % b], writes=["vr%d" % s])
                T.add("sp", dma(vr_s[(t * 4 + s) * 128:(t * 4 + s + 1) * 128, :], vr[:, s, :]), reads=["vr%d" % s], dma=True)

            def gen_qf():
                pend = None

                def finish_qk(p):
                    (b, i, g, gcol, dstd) = p
                    b2 = nb()
                    T.add("pe", mm(PB[b2][:, :], bonesb, sqf[i], True, True), reads=["bonesb", "sqf%d" % i], writes=["pb%d" % b2])
                    T.add("act", act(rsf[i], PB[b2][:, :], AF.Sqrt, bias=EPS, scale=1.0 / 64), reads=["pb%d" % b2], writes=["rsf%d" % i])
                    T.add("dve", rcp(rsf[i], rsf[i]), reads=["rsf%d" % i], writes=["rsf%d" % i])
                    k = next_stg()
                    T.add("dve", stt(stg[k], PB[b][:, :], gcol, rsf[i], ALU.mult, ALU.mult),
                          reads=["pb%d" % b, "rsf%d" % i, "cst"], writes=["stg%d" % k])
                    T.add("sp", dma(dstd[g * 128:(g + 1) * 128, tc0:tc0 + 512], stg[k]), reads=["stg%d" % k], dma=True)

                for idx in range(8):
                    g = idx % 4
                    col = (C_QF if idx < 4 else C_KF) + g * 128
                    b = nb()
                    proj_fm(col, b, win, WIN_RES)
                    i = idx % 2
                    T.add("act", act(sqf[i], PB[b][:, :], AF.Square), reads=["pb%d" % b], writes=["sqf%d" % i])
                    if pend is not None:
                        finish_qk(pend)
                    pend = (b, i, g, gq if idx < 4 else gk, qf_s if idx < 4 else kf_s)
                    yield
                finish_qk(pend)
                yield
            def gen_af():
                for g in range(8):
                    b = nb()
                    proj_fm(C_AF + g * 128, b, win, WIN_RES)
                    k = next_stg()
                    T.add("act", act(stg[k], PB[b][:, :], AF.Sigmoid), reads=["pb%d" % b], writes=["stg%d" % k])
                    T.add("sp", dma(gaf_s[g * 128:(g + 1) * 128, tc0:tc0 + 512], stg[k]), reads=["stg%d" % k], dma=True)
                    yield
            def gen_tm():
                for s in range(4):
                    J = t * 4 + s
                    b = nb()
                    for c in range(8):
                        T.add("pe", mm(PB[b][:, :], hT[:, c, s * 128:(s + 1) * 128], win[:, c, C_VF:C_VF + 512], c == 0, c == 7),
                              reads=WIN_RES + [HT_RES[s]], writes=["pb%d" % b])
                    kv = J % 2
                    T.add("act", acp(stgv[kv][:, :, 0:64], PB[b][:, :].rearrange("p (h d) -> p h d", h=8)),
                          reads=["pb%d" % b], writes=["stgv%d" % kv])
                    T.add("sp", dma(vf_s[J * 128:(J + 1) * 128, :], stgv[kv].rearrange("p h d -> p (h d)")),
                          reads=["stgv%d" % kv], dma=True)
                    yield
                    b = nb()
                    for c in range(8):
                        T.add("pe", mm(PB[b][:, 0:8], hT[:, c, s * 128:(s + 1) * 128], win[:, c, C_FF:C_FF + 8], c == 0, c == 7),
                              reads=WIN_RES + [HT_RES[s]], writes=["pb%d" % b])
                    T.add("dve", tt(fb, PB[b][:, 0:8], bfor, ALU.add), reads=["pb%d" % b, "cst"], writes=["fb"])
                    T.add("act", act(fe, fb, AF.Exp, scale=-1.0), reads=["fb"], writes=["fe"])
                    T.add("act", act(fsp, fe, AF.Ln, bias=1.0), reads=["fe"], writes=["fsp"])
                    yield
                    b = nb()
                    T.add("pe", mm(PB[b][:, 0:8], negtri, fsp, True, True), reads=["cst", "fsp"], writes=["pb%d" % b])
                    T.add("pe", mm(PB[b][:, 8:16], negones, fsp, True, True), reads=["negones", "fsp"], writes=["pb%d" % b])
                    ca, cb = carry[J % 2], carry[(J + 1) % 2]
                    car, cbr = "carry%d" % (J % 2), "carry%d" % ((J + 1) % 2)
                    if s == 0:
                        T.add("dve", cp(mref[:, t, :], ca), reads=[car], writes=["mref"])
                    T.add("dve", tt(cK[:, J, :], PB[b][:, 0:8], ca, ALU.add), reads=["pb%d" % b, car], writes=["cK"])
                    T.add("dve", tt(cb, PB[b][:, 8:16], ca, ALU.add), reads=["pb%d" % b, car], writes=[cbr])
                    T.add("dve", tt(d8, cK[:, J, :], mref[:, t, :], ALU.subtract), reads=["cK", "mref"], writes=["d8"])
                    T.add("dve", ts(d8, d8, 8.0, ALU.mult), reads=["d8"], writes=["d8"])
                    T.add("dve", cp(hml[:, 0:8], d8), reads=["d8"], writes=["hml"])
                    T.add("dve", tt(r1, d8, hml[:, 0:8], ALU.subtract), reads=["d8", "hml"], writes=["r1"])
                    T.add("dve", cp(hml[:, 8:16], r1), reads=["r1"], writes=["hml"])
                    T.add("dve", tt(r2, r1, hml[:, 8:16], ALU.subtract), reads=["r1", "hml"], writes=["r2"])
                    T.add("dve", cp(hml[:, 16:24], r2), reads=["r2"], writes=["hml"])
                    yield
                    b = nb()
                    T.add("pe", tr(PBb[b][0:24, 0:128], hml, identb), reads=["hml", "identb"], writes=["pb%d" % b])
                    T.add("dve", cp(cqst[0:24, s * 128:(s + 1) * 128], PBb[b][0:24, 0:128]), reads=["pb%d" % b], writes=["cqst"])
                    yield
                T.add("sp", dma(cq_s[:, tc0:tc0 + 512], cqst[0:24, :]), reads=["cqst"], dma=True)
            def gen_ar():
                for g in range(8):
                    b = nb()
                    proj_fm(C_AR + g * 128, b, win, WIN_RES)
                    k = next_stg()
                    T.add("act", act(stg[k], PB[b][:, :], AF.Sigmoid), reads=["pb%d" % b], writes=["stg%d" % k])
                    T.add("sp", dma(sar_s[g * 128:(g + 1) * 128, tc0:tc0 + 512], stg[k]), reads=["stg%d" % k], dma=True)
                    yield

            def chain(*gs):
                for g_ in gs:
                    yield from g_

            def gen_norm():
                if t + 1 < NT:
                    for s in range(4):
                        norm_transpose(xb[s % 2], "xb%d" % (s % 2), s, gmt, "gmt", hTb[(t + 1) % 2], HTR[(t + 1) % 2])
                        load_xsub(xb[s % 2], "xb%d" % (s % 2), x_d, (t + 1) * 4 + s + 2)
                        yield

            def gen_pre():
                if t + 1 < NT:
                    for s in range(4):
                        yield
                        norm_pre(xb[s % 2], "xb%d" % (s % 2), s, gmt, "gmt", 4)
                        load_xsub(xb[s % 2], "xb%d" % (s % 2), x_d, (t + 1) * 4 + s + 2)
                        yield

            gens = [chain(gen_qf(), gen_af(), gen_ar()), gen_tm(), gen_pre()]
            alive = [True, True, True]
            while any(alive):
                for gi in (0, 0, 1, 2):
                    if alive[gi]:
                        alive[gi] = next(gens[gi], 'done') != 'done'

        T.barrier()
        n_ops_A = len(T.ops)
        A.off = persist_end
        KT = A.alloc([4, S], BF16)
        V = A.alloc([NJ, 520], BF16)
        qTb = [A.alloc([8, 512], BF16) for _ in range(2)]
        cqb = [A.alloc([512], BF16) for _ in range(2)]
        bias = A.alloc([NJ, 8], F32)
        pbuf = [A.alloc([512], BF16) for _ in range(6)]
        den = A.alloc([512], F32)
        bcs = A.alloc([512], F32)
        ostg = [A.alloc([512], BF16) for _ in range(2)]
        qrz = A.alloc([4, 512], BF16)
        krT = A.alloc([2, 512], BF16)
        qxT = A.alloc([2, 512], BF16)
        gtT = A.alloc([4, 512], BF16)
        vr = A.alloc([4, 512], BF16)
        kz = A.alloc([256], BF16)
        sT = A.alloc([512], BF16)
        stf = A.alloc([2, 128], F32)
        stb = A.alloc([4, 128], BF16)
        osb = A.alloc([512], F32)
        sqo = A.alloc([512], F32)
        s1 = A.alloc([4], F32)
        s2 = A.alloc([4], F32)
        mean = A.alloc([4], F32)
        msq = A.alloc([4], F32)
        var = A.alloc([4], F32)
        rstd = A.alloc([4], F32)
        mhalf = A.alloc([4], F32)
        on = A.alloc([512], BF16)
        ofT = A.alloc([4, 512], BF16)
        T.add("pool", mset(stf, 0.0), writes=["stf"])
        T.add("pool", mset(stb, 0.0), writes=["stb"])
        T.add("pool", mset(qrz, 0.0), writes=["qrze", "qrzo"])
        T.add("pool", mset(mhalf, -0.5), writes=["mhalf"])
        qrz4 = qrz.rearrange("p (c two) n -> p c two n", two=2)
        qr4 = qr_s.rearrange("(c two d) s -> d c two s", two=2, d=64)

        for i_ in range(2):
            T.add("pool", mset(qTb[i_], 0.0), writes=["qTe%d" % i_, "qTo%d" % i_])
            T.add("pool", mset(cqb[i_], 0.0), writes=["cq%d" % i_])
        qT4b = [q_.rearrange("p (c two) n -> p c two n", two=2) for q_ in qTb]

        def load_q(G_):
            if G_ < NT:
                i_ = G_ % 2
                c0_ = G_ * 512
                T.add("sp", dma(qT4b[i_][0:64, :, 0, :], qf4[:, :, 0, c0_:c0_ + 512]), writes=["qTe%d" % i_], dma=True)
                T.add("sp", dma(qT4b[i_][64:128, :, 1, :], qf4[:, :, 1, c0_:c0_ + 512]), writes=["qTo%d" % i_], dma=True)
                T.add("sp", dma(cqb[i_][0:24, :], cq_s[:, c0_:c0_ + 512]), writes=["cq%d" % i_], dma=True)

        qf4 = qf_s.rearrange("(c two d) s -> d c two s", two=2, d=64)
        KT_RES = ["KT%d" % hc for hc in range(4)]
        for hc in range(4):
            T.add("sp", dma(KT[:, hc, :], kf_s[hc * 128:(hc + 1) * 128, :]), writes=[KT_RES[hc]], dma=True)
        vf_v = vf_s.rearrange("(j p) f -> p j f", p=128)
        V_RES = []
        for j0 in range(0, NJ, 8):
            j1 = min(NJ, j0 + 8)
            T.add("sp", dma(V[:, j0:j1, :], vf_v[:, j0:j1, :]), writes=["V%d" % j0], dma=True)
            V_RES.append("V%d" % j0)

        LA = 2
        SB = [0, 1, 2]
        def gen_ret_all():
            for RG in range(NT):
                rt0 = RG * 512
                v2 = lambda d_: d_.rearrange("(c p) s -> p c s", p=128)[:, :, rt0:rt0 + 512]
                T.add("sp", dma(qrz4[0:64, :, 0, :], qr4[:, :, 0, rt0:rt0 + 512]), writes=["qrze"], dma=True)
                T.add("sp", dma(qrz4[64:128, :, 1, :], qr4[:, :, 1, rt0:rt0 + 512]), writes=["qrzo"], dma=True)
                T.add("sp", dma(krT, v2(kr_s)), writes=["krT"], dma=True)
                T.add("sp", dma(qxT, v2(qx_s)), writes=["qxT"], dma=True)
                T.add("sp", dma(gtT, v2(gt_s)), writes=["gtT"], dma=True)
                T.add("sp", dma(vr, vr_s.rearrange("(j p) f -> p j f", p=128)[:, RG * 4:RG * 4 + 4, :]), writes=["vr"], dma=True)
                yield
                for s in range(4):
                    cs_ = slice(s * 128, (s + 1) * 128)
                    for c2 in range(2):
                        T.add("pe", tr(PBb[7][:, c2 * 128:(c2 + 1) * 128], krT[:, c2, cs_], identb),
                              reads=["krT", "identb"], writes=["pb7"])
                    T.add("dve", tt(kz.rearrange("p (h d) -> p h d", h=4), PBb[7][:, 0:256].rearrange("p (h d) -> p h d", h=4),
                                    zeta.unsqueeze(2).broadcast_to([128, 4, 64]), ALU.mult),
                          reads=["pb7", "cst"], writes=["kz"])
                    for h in range(4):
                        c2 = h // 2
                        T.add("pe", mm(PB[6][:, h * 128:(h + 1) * 128], krT[:, c2, cs_], qrz[:, h, cs_], True, True),
                              reads=["krT", "qrze" if h % 2 == 0 else "qrzo"], writes=["pb6"])
                    yield
                    T.add("dve", tt(sT, PB[6][:, :], decay, ALU.mult), reads=["pb6", "cst"], writes=["sT"])
                    yield
                    for h in range(4):
                        c2 = h // 2
                        T.add("pe", mm(PB[6][:, h * 128:(h + 1) * 128], sT[:, h * 128:(h + 1) * 128], vr[:, s, h * 128:(h + 1) * 128], True, False),
                              reads=["sT", "vr"], writes=["pb6"])
                        T.add("pe", mm(PB[6][:, h * 128:(h + 1) * 128], qxT[:, c2, cs_], stb[:, h, :], False, True),
                              reads=["qxT", "stb"], writes=["pb6"])
                    for h in range(4):
                        c2, po = h // 2, (h % 2) * 64
                        T.add("pe", mm(PB[7][po:po + 64, c2 * 128:(c2 + 1) * 128], kz[:, h * 64:(h + 1) * 64], vr[:, s, h * 128:(h + 1) * 128], True, True),
                              reads=["kz", "vr"], writes=["pb7"])
                    yield
                    T.add("dve", cp(osb, PB[6][:, :]), reads=["pb6"], writes=["osb"])
                    for c2 in range(2):
                        T.add("dve", stt(stf[:, c2, :], stf[:, c2, :], gch[:, c2:c2 + 1], PB[7][:, c2 * 128:(c2 + 1) * 128], ALU.mult, ALU.add),
                              reads=["stf", "pb7", "cst"], writes=["stf"])
                    for c2 in range(2):
                        T.add("pool", cp(stb[0:64, 2 * c2, :], stf[0:64, c2, :]), reads=["stf"], writes=["stb"])
                        T.add("pool", cp(stb[64:128, 2 * c2 + 1, :], stf[64:128, c2, :]), reads=["stf"], writes=["stb"])
                    T.add("pool", tt(sqo, osb, osb, ALU.mult), reads=["osb"], writes=["sqo"])
                    T.add("dve", (lambda o, i_: (lambda e: e.tensor_reduce(out=o, in_=i_, axis=AX.X, op=ALU.add)))(
                        s1, osb.rearrange("p (h e) -> p h e", h=4)), reads=["osb"], writes=["s1"])
                    T.add("dve", ts(mean, s1, 1.0 / 128, ALU.mult), reads=["s1"], writes=["mean"])
                    T.add("dve", tt(msq, mean, mean, ALU.mult), reads=["mean"], writes=["msq"])
                    yield
                    T.add("dve", (lambda o, i_: (lambda e: e.tensor_reduce(out=o, in_=i_, axis=AX.X, op=ALU.add)))(
                        s2, sqo.rearrange("p (h e) -> p h e", h=4)), reads=["sqo"], writes=["s2"])
                    T.add("dve", stt(var, s2, 1.0 / 128, msq, ALU.mult, ALU.subtract), reads=["s2", "msq"], writes=["var"])
                    T.add("dve", ts(var, var, EPS, ALU.add), reads=["var"], writes=["var"])
                    T.add("pool", tt(rstd, var, mhalf, ALU.pow), reads=["var", "mhalf"], writes=["rstd"])
                    yield
                    for h in range(4):
                        T.add("dve", ts(on[:, h * 128:(h + 1) * 128], osb[:, h * 128:(h + 1) * 128], mean[:, h:h + 1], ALU.subtract,
                                        rstd[:, h:h + 1], ALU.mult),
                              reads=["osb", "mean", "rstd"], writes=["on"])
                    yield
                    for h in range(4):
                        T.add("pe", tr(PBb[7][:, 512 + h * 128:512 + (h + 1) * 128], on[:, h * 128:(h + 1) * 128], identb),
                              reads=["on", "identb"], writes=["pb7"])
                    for h in range(4):
                        T.add("dve", stt(ofT[:, h, cs_], PBb[7][:, 512 + h * 128:512 + (h + 1) * 128], gret[:, h:h + 1], gtT[:, h, cs_], ALU.mult, ALU.mult),
                              reads=["pb7", "gtT", "cst"], writes=["ofT"])
                    yield
                T.add("sp", dma(v2(ofr_s), ofT), reads=["ofT"], dma=True)

        g_ret = gen_ret_all()
        ret_state = {'alive': True, 'cnt': 0}
        tot_blocks = sum(8 * (4 * g_ + 4) for g_ in range(NT))
        ret_every = max(1, tot_blocks // (NT * 33 + 8))
        for G in range(NT):
            tc0 = G * 512
            nJ = 4 * G + 4
            if G == 0:
                load_q(0)
            load_q(G + 1)
            qT = qTb[G % 2]
            cq = cqb[G % 2]
            gq_ = G % 2
            T.add("dve", tt(bias[:, 0:nJ, :], mref[:, G, :].unsqueeze(1).broadcast_to([128, nJ, 8]), cK[:, 0:nJ, :], ALU.subtract),
                  reads=["mref", "cK"], writes=["bias"])
            seq = [(h, J) for h in range(8) for J in range(nJ)]
            deferred = []

            def emit_qk(i, h, J):
                hc = h // 2
                q0 = max(0, J - 4 * G) * 128
                b = SB[i % 3]
                pk = i % 6
                diag = J >= 4 * G
                T.add("pe", mm(PB[b][:, q0:512], KT[:, hc, J * 128:(J + 1) * 128], qT[:, h, q0:512], True, False),
                      reads=[KT_RES[hc], ("qTe%d" if h % 2 == 0 else "qTo%d") % gq_], writes=["pb%d" % b])
                T.add("pe", mm(PB[b][:, q0:512], eselb[:, h, :], cq[:, q0:512], False, not diag),
                      reads=["eselb", "cq%d" % gq_], writes=["pb%d" % b])
                if diag:
                    T.add("pe", mm(PB[b][:, q0:q0 + 128], identb, negmask, False, True),
                          reads=["identb", "negmask"], writes=["pb%d" % b])
                T.add("act", act(pbuf[pk][:, q0:512], PB[b][:, q0:512], AF.Exp, bias=bias[:, J, h:h + 1], scale=0.125),
                      reads=["pb%d" % b, "bias"], writes=["pbuf%d" % pk])

            def emit_pv(i, h, J):
                q0 = max(0, J - 4 * G) * 128
                pk = i % 6
                ob = 3 + h % 2
                T.add("pe", mm(PB[ob][0:65, q0:512], V[:, J, h * 65:(h + 1) * 65], pbuf[pk][:, q0:512], J == 0, J == nJ - 1),
                      reads=["V%d" % (J // 8 * 8), "pbuf%d" % pk], writes=["pb%d" % ob])

            def epi1(h):
                ob = 3 + h % 2
                T.add("dve", cp(den[64:65, :], PB[ob][64:65, :]), reads=["pb%d" % ob], writes=["den"])
                T.add("dve", rcp(den[64:65, :], den[64:65, :]), reads=["den"], writes=["den"])

            def epi2(h):
                ob = 3 + h % 2
                k = h % 2
                T.add("pe", mm(PB[5][0:64, :], onesf[64:65, 0:64], den[64:65, :], True, True), reads=["onesf", "den"], writes=["pb5"])
                T.add("dve", cp(bcs[0:64, :], PB[5][0:64, :]), reads=["pb5"], writes=["bcs"])
                T.add("dve", tt(ostg[k][0:64, :], PB[ob][0:64, :], bcs[0:64, :], ALU.mult), reads=["pb%d" % ob, "bcs"], writes=["ostg%d" % k])
                T.add("sp", dma(of_s[h * 64:(h + 1) * 64, tc0:tc0 + 512], ostg[k][0:64, :]), reads=["ostg%d" % k], dma=True)

            n = len(seq)
            for i in range(n + LA + 3):
                if i < n:
                    ret_state['cnt'] += 1
                    if ret_state['alive'] and ret_state['cnt'] % ret_every == 0:
                        ret_state['alive'] = next(g_ret, 'done') != 'done'
                    emit_qk(i, *seq[i])
                j = i - LA
                if 0 <= j < n:
                    h, J = seq[j]
                    emit_pv(j, h, J)
                    if J == nJ - 1:
                        epi1(h)
                        deferred.append((i + 2, h))
                while deferred and deferred[0][0] <= i:
                    epi2(deferred.pop(0)[1])
            assert not deferred

        while ret_state['alive']:
            ret_state['alive'] = next(g_ret, 'done') != 'done'

        T.barrier()
        n_ops_B = len(T.ops)
        A.off = phaseC_start
        wg = A.alloc([8, DFF], BF16)
        wu = A.alloc([8, DFF], BF16)
        b2_start = A.off
        wfo = A.alloc([4, D], BF16)
        wro = A.alloc([4, D], BF16)
        wout = A.alloc([8, D], BF16)
        on8 = [A.alloc([4, 512], BF16) for _ in range(2)]
        ofr = [A.alloc([4, 512], BF16) for _ in range(2)]
        gaf = [A.alloc([8, 512], BF16) for _ in range(2)]
        sar = [A.alloc([8, 512], BF16) for _ in range(2)]
        tmpf = [A.alloc([512], F32) for _ in range(2)]
        tmpr = [A.alloc([512], F32) for _ in range(2)]
        mgT = A.alloc([8, 512], BF16)
        xb2 = [A.alloc([D], F32) for _ in range(2)]
        b2_end = A.off
        T.add("pool", dma(wfo, wfo_d.rearrange("(c p) n -> p c n", p=128)), writes=["wfo"], dma=True)
        T.add("pool", dma(wro, wro_d.rearrange("(c p) n -> p c n", p=128)), writes=["wro"], dma=True)
        T.add("pool", dma(wout, wout_d.rearrange("(c p) n -> p c n", p=128)), writes=["wout"], dma=True)
        wg_v = wg_d.rearrange("(c p) n -> p c n", p=128)
        wu_v = wu_d.rearrange("(c p) n -> p c n", p=128)
        WG_RES, WU_RES = [], []
        for n0 in range(0, DFF, 1408):
            T.add("pool", dma(wg[:, :, n0:n0 + 1408], wg_v[:, :, n0:n0 + 1408]), writes=["wg%d" % n0], dma=True)
            T.add("pool", dma(wu[:, :, n0:n0 + 1408], wu_v[:, :, n0:n0 + 1408]), writes=["wu%d" % n0], dma=True)
            WG_RES.append("wg%d" % n0)
            WU_RES.append("wu%d" % n0)
        def load_b2(G):
            if G < NT:
                c0_ = G * 512
                T.add("sp", dma(on8[G % 2], of_s.rearrange("(c p) s -> p c s", p=128)[:, :, c0_:c0_ + 512]), writes=["on8%d" % (G % 2)], dma=True)
                T.add("sp", dma(gaf[G % 2], gaf_s.rearrange("(c p) s -> p c s", p=128)[:, :, c0_:c0_ + 512]), writes=["gaf%d" % (G % 2)], dma=True)
                T.add("sp", dma(ofr[G % 2], ofr_s.rearrange("(c p) s -> p c s", p=128)[:, :, c0_:c0_ + 512]), writes=["ofr%d" % (G % 2)], dma=True)
                T.add("sp", dma(sar[G % 2], sar_s.rearrange("(c p) s -> p c s", p=128)[:, :, c0_:c0_ + 512]), writes=["sar%d" % (G % 2)], dma=True)

        load_b2(0)
        for G in range(NT):
            tc0 = G * 512
            load_b2(G + 1)
            for dg in range(8):
                b = (2 * dg) % 8
                b2_ = (2 * dg + 1) % 8
                k = dg % 2
                for c in range(4):
                    T.add("pe", mm(PB[b][:, :], wfo[:, c, dg * 128:(dg + 1) * 128], on8[G % 2][:, c, :], c == 0, c == 3),
                          reads=["wfo", "on8%d" % (G % 2)], writes=["pb%d" % b])
                for c in range(4):
                    T.add("pe", mm(PB[b2_][:, :], wro[:, c, dg * 128:(dg + 1) * 128], ofr[G % 2][:, c, :], c == 0, c == 3),
                          reads=["wro", "ofr%d" % (G % 2)], writes=["pb%d" % b2_])
                T.add("dve", tt(tmpf[k], PB[b][:, :], gaf[G % 2][:, dg, :], ALU.mult), reads=["pb%d" % b, "gaf%d" % (G % 2)], writes=["tmpf%d" % k])
                T.add("dve", tt(tmpr[k], PB[b2_][:, :], sar[G % 2][:, dg, :], ALU.mult), reads=["pb%d" % b2_, "sar%d" % (G % 2)], writes=["tmpr%d" % k])
                T.add("pool", tt(mgT[:, dg, :], tmpf[k], tmpr[k], ALU.add), reads=["tmpf%d" % k, "tmpr%d" % k], writes=["mgT"])
            for s in range(4):
                load_xsub(xb2[s % 2], "xb2%d" % (s % 2), x_d, G * 4 + s)
                for half in range(2):
                    b = (2 * s + half) % 8
                    for c in range(8):
                        T.add("pe", mm(PB[b][:, :], mgT[:, c, s * 128:(s + 1) * 128], wout[:, c, half * 512:(half + 1) * 512], c == 0, c == 7),
                              reads=["mgT", "wout"], writes=["pb%d" % b])
                    T.add("dve", tt(xb2[s % 2][:, half * 512:(half + 1) * 512], PB[b][:, :], xb2[s % 2][:, half * 512:(half + 1) * 512], ALU.add),
                          reads=["pb%d" % b, "xb2%d" % (s % 2)], writes=["xb2%d" % (s % 2)])
                r0 = (G * 4 + s) * 128
                T.add("sp", dma(out_d[r0:r0 + 128, :], xb2[s % 2]), reads=["xb2%d" % (s % 2)], dma=True)

        T.barrier()
        n_ops_B2 = len(T.ops)
        A.off = b2_start
        wd = A.alloc([22, D], BF16)
        wd_v = wd_d.rearrange("(c p) n -> p c n", p=128)
        WD_RES = []
        for c0 in range(0, 22, 11):
            T.add("pool", dma(wd[:, c0:c0 + 11, :], wd_v[:, c0:c0 + 11, :]), writes=["wd%d" % c0], dma=True)
            WD_RES.append("wd%d" % c0)
        gft = A.alloc([D], F32)
        xc = [A.alloc([D], F32) for _ in range(2)]
        xr = [A.alloc([D], F32) for _ in range(2)]
        junk = A.alloc([D], BF16)
        ss = A.alloc([4], F32)
        rs = A.alloc([4], F32)
        xn = A.alloc([2, D], BF16)
        hTb = [A.alloc([8, 512], BF16) for _ in range(2)]
        sg = [A.alloc([512], F32) for _ in range(2)]
        hid = A.alloc([22, 512], BF16)

        pass
        T.add("sp", dma(gft, gft_d), writes=["gft"], dma=True)

        pbc[0] = 0
        load_xsub(xc[0], "xc0", out_d, 0)
        load_xsub(xc[1], "xc1", out_d, 1)
        for s in range(4):
            norm_transpose(xc[s % 2], "xc%d" % (s % 2), s, gft, "gft", hTb[0], HTR[0])
            load_xsub(xc[s % 2], "xc%d" % (s % 2), out_d, s + 2)
        for t in range(NT):
            hT = hTb[t % 2]
            HT_RES = HTR[t % 2]
            for fc in range(22):
                if t + 1 < NT and fc in (0, 4, 8, 12):
                    s = (0, 4, 8, 12).index(fc)
                    norm_pre(xc[s % 2], "xc%d" % (s % 2), s, gft, "gft", 2)
                    load_xsub(xc[s % 2], "xc%d" % (s % 2), out_d, (t + 1) * 4 + s + 2)
                if t + 1 < NT and fc in (3, 7, 11, 15):
                    s = (3, 7, 11, 15).index(fc)
                    norm_tr(s, hTb[(t + 1) % 2], HTR[(t + 1) % 2], 2)
                ba = nb(6)
                proj_fm(fc * 128, ba, wg, WG_RES)
                bb = nb(6)
                proj_fm(fc * 128, bb, wu, WU_RES)
                k = fc % 2
                T.add("act", act(sg[k], PB[ba][:, :], AF.Silu), reads=["pb%d" % ba], writes=["sg%d" % k])
                T.add("dve", tt(hid[:, fc, :], PB[bb][:, :], sg[k], ALU.mult), reads=["pb%d" % bb, "sg%d" % k], writes=["hid"])
            for s in range(4):
                k = s % 2
                load_xsub(xr[k], "xr%d" % k, out_d, t * 4 + s)
                for half in range(2):
                    b = 6 + half
                    for fc in range(22):
                        T.add("pe", mm(PB[b][:, :], hid[:, fc, s * 128:(s + 1) * 128], wd[:, fc, half * 512:(half + 1) * 512], fc == 0, fc == 21),
                              reads=["hid"] + WD_RES, writes=["pb%d" % b])
                    T.add("dve", tt(xr[k][:, half * 512:(half + 1) * 512], PB[b][:, :], xr[k][:, half * 512:(half + 1) * 512], ALU.add),
                          reads=["pb%d" % b, "xr%d" % k], writes=["xr%d" % k])
                r0 = (t * 4 + s) * 128
                T.add("sp", dma(out_d[r0:r0 + 128, :], xr[k]), reads=["xr%d" % k], dma=True)

        def sem_alloc(name):
            return st.enter_context(nc.semaphore(name))

        _stop = os.environ.get("K_STOP", "")
        if _stop:
            ncut = {"A": n_ops_A, "B": n_ops_B, "B2": n_ops_B2}.get(_stop, int(_stop) if _stop.isdigit() else len(T.ops))
            del T.ops[ncut:]
            T.dma_count = {}
            for op in T.ops:
                if op["dma"]:
                    T.dma_count[op["key"]] = T.dma_count.get(op["key"], 0) + 1
        run = T.prepare(sem_alloc)
        with nc.Block() as block:
            block.sync(lambda e: run("sp", e))
            block.tensor(lambda e: run("pe", e))
            block.scalar(lambda e: run("act", e))
            block.vector(lambda e: run("dve", e))
            block.gpsimd(lambda e: run("pool", e))
    nc._dbg_iname = T.iname
    return nc


def host_constants(S):
    f64 = np.float64
    pos = np.arange(S, dtype=np.float32)
    half = 32
    inv_freq = (1.0 / (np.float32(10000.0) ** (np.arange(half, dtype=np.float32) / np.float32(half)))).astype(np.float32)
    p = np.arange(128)
    d = p % 64
    ang = (pos[None, :] * inv_freq[d % 32][:, None]).astype(np.float32).astype(f64)
    cos = np.cos(ang)
    sin = np.sin(ang) * np.where(d < 32, -1.0, 1.0)[:, None]
    rot = np.stack([cos, sin, 0.125 * cos, 0.125 * sin]).astype(np.float32)
    log_g = np.log1p(-(2.0 ** (-5.0 - np.arange(4, dtype=f64))))
    i = np.arange(128)
    diff = i[None, :] - i[:, None]
    decay = np.zeros((128, 4, 128), f64)
    for h in range(4):
        decay[:, h, :] = np.where(diff >= 0, np.exp(np.maximum(diff, 0) * log_g[h]), 0.0)
    xi = np.zeros((128, 2, 512), f64)
    for c2 in range(2):
        hd = 2 * c2 + p // 64
        xi[:, c2, :] = np.exp((np.arange(512) % 128 + 1.0)[None, :] * log_g[hd][:, None])
    zeta = np.exp((127.0 - i)[:, None] * log_g[None, :])
    gch = np.stack([np.exp(128.0 * log_g[2 * c2 + p // 64]) for c2 in range(2)], axis=1)
    ident = np.eye(128)
    negtri = np.where(i[:, None] <= i[None, :], -1.0, 0.0)
    bones = np.kron(np.eye(2), np.ones((64, 64)))
    esel = np.zeros((24, 8, 128), np.float32)
    for h in range(8):
        for k in range(3):
            esel[8 * k + h, h, :] = 1.0
    return rot, ident, negtri, bones, decay.reshape(128, 512), xi.reshape(128, 1024), zeta, gch, esel.reshape(24, 1024)


def win_perm():
    def swap(base):
        idx = []
        for h in range(4):
            idx += list(range(base + h * 64 + 32, base + h * 64 + 64)) + list(range(base + h * 64, base + h * 64 + 32))
        return idx
    r = lambda a, n: list(range(a, a + n))
    perm = (r(0, 256) + swap(0) + r(256, 256) + swap(256) + r(1024, 512) + r(1536, 512) + r(2048, 512)
            + r(3080, 1024) + r(4104, 1024) + r(512, 512) + r(2560, 512) + r(3072, 8))
    assert len(perm) == NWIN
    return np.array(perm)


def make_in_maps(S, x, g_mix, w_in, b_forget, g_ret_norm, w_ret_o, g_fox_q, g_fox_k, w_fox_o,
                 w_out, g_ffn, w_gate, w_up, w_down):
    f = lambda a: np.ascontiguousarray(np.asarray(a, dtype=np.float32))
    rot, ident, negtri, bones, decay, xi, zeta, gch, esel = host_constants(S)
    cst = np.zeros((128, NCST), np.float32)
    cst[:, 0:128] = ident
    cst[:, 128:256] = negtri
    cst[:, 256:384] = bones
    cst[:, 384:896] = decay
    cst[:, 896:1920] = xi
    cst[:, 1920:1924] = f(g_ret_norm)[0].reshape(4, 128).T
    cst[:, 1924] = np.tile(f(g_fox_q)[0], 2)
    cst[:, 1925] = np.tile(f(g_fox_k)[0], 2)
    cst[:, 1926:1928] = gch
    cst[:, 1928:1932] = zeta
    cst[:, 1932:1940] = np.broadcast_to(f(b_forget)[0][None, :], (128, 8))
    shared = {
        "win": f(f(w_in)[0][:, win_perm()]),
        "wro": f(w_ret_o)[0], "wfo": f(w_fox_o)[0], "wout": f(w_out)[0],
        "wg": f(w_gate)[0], "wu": f(w_up)[0], "wd": f(w_down)[0],
        "gmt": f(np.broadcast_to(f(g_mix)[0][None, :], (128, D))),
        "gft": f(np.broadcast_to(f(g_ffn)[0][None, :], (128, D))),
        "cst": cst, "esel": f(esel), "rot": f(rot),
    }
    xs = f(x)
    return [dict(shared, x=np.ascontiguousarray(xs[b])) for b in range(xs.shape[0])]


_NC_CACHE = {}


def kernel(**inputs):
    x = np.asarray(inputs["x"])
    B, S, _ = x.shape
    if S not in _NC_CACHE:
        _NC_CACHE[S] = build_nc(S)
    nc = _NC_CACHE[S]
    in_maps = make_in_maps(S, **inputs)
    res = run_bass_kernel_spmd(nc, in_maps, core_ids=list(range(B)))
    return np.stack([np.asarray(r["out"]) for r in res.results], axis=0).astype(np.float32)
```
